# Optimizing a Trainium2 kernel written in Bass

```python
import functools
import jax, jax.numpy as jnp
from jax import lax
import numpy as np

D_MODEL = 1024
BATCH = 8
SEQ = 4096
DEPTH = 2

HEAD_DIM = 64
GROUP_WIDTH = D_MODEL // 4
MIX_WIDTH = 4 * GROUP_WIDTH
N_GROUP_HEADS = GROUP_WIDTH // HEAD_DIM
SHORT_CONV_K = 3
DSWA_CONFIGS = ((128, 1), (512, 4), (2048, 16))
POOL_WINDOWS = (2, 4, 8, 16)
POOL_GROUP = GROUP_WIDTH // len(POOL_WINDOWS)
ROPE_THETA = 500000.0
ROPE_DIM = HEAD_DIM // 4
Q_BLOCK = 128
MEM_LEN = 256
XA_HEADS = 4
XA_HEAD_DIM = D_MODEL // XA_HEADS
D_FF = ((8 * D_MODEL // 3 + 127) // 128) * 128
FFN_CONV_K = 3
RMS_EPS = 1e-6
NEG_INF = -1e30
FORGET_BIAS_INIT = 3.0
N_IN = 3 * GROUP_WIDTH + 3 * GROUP_WIDTH + 3 * GROUP_WIDTH + N_GROUP_HEADS + GROUP_WIDTH

kernel_name = "hybrid_parallel_heads_conv_dilated_fox_pool"


def rms_norm(t, g):
    tf = t.astype(jnp.float32)
    y = tf * lax.rsqrt(jnp.mean(tf * tf, axis=-1, keepdims=True) + RMS_EPS)
    return (y * g.astype(jnp.float32)).astype(t.dtype)


def causal_depthwise_conv(t, w):
    K, C = w.shape
    return lax.conv_general_dilated(
        t, w.astype(t.dtype)[:, None, :], window_strides=(1,), padding=((K - 1, 0),),
        dimension_numbers=("NWC", "WIO", "NWC"), feature_group_count=C)


def to_heads(t):
    B, S, W = t.shape
    return t.reshape(B, S, W // HEAD_DIM, HEAD_DIM).transpose(0, 2, 1, 3)


def from_heads(t):
    B, H, S, dh = t.shape
    return t.transpose(0, 2, 1, 3).reshape(B, S, H * dh)


def rope_tables(positions):
    inv_freq = ROPE_THETA ** (-jnp.arange(0, ROPE_DIM, 2, dtype=jnp.float32) / ROPE_DIM)
    ang = positions.astype(jnp.float32)[:, None, :, None] * inv_freq
    return jnp.cos(ang), jnp.sin(ang)


def apply_partial_rope(t, cos, sin):
    tf = t.astype(jnp.float32)
    r1, r2 = jnp.split(tf[..., :ROPE_DIM], 2, axis=-1)
    rot = jnp.concatenate([r1 * cos - r2 * sin, r2 * cos + r1 * sin], axis=-1)
    return jnp.concatenate([rot, tf[..., ROPE_DIM:]], axis=-1).astype(t.dtype)


def short_conv_mixer(p, w_conv):
    h, b_gate, c_gate = jnp.split(p, 3, axis=-1)
    return b_gate * causal_depthwise_conv(c_gate * h, w_conv)


def dilated_attention(q, k, v):
    B, H, S, dh = q.shape
    nb = S // Q_BLOCK
    qb = q.reshape(B, H, nb, Q_BLOCK, dh).transpose(2, 0, 1, 3, 4)
    kf = k.astype(jnp.float32)
    vf = v.astype(jnp.float32)
    scale = dh ** -0.5

    def block(args):
        i, qi = args
        t = i * Q_BLOCK + jnp.arange(Q_BLOCK)
        qs = qi.astype(jnp.float32) * scale
        scores, values = [], []
        for window, dil in DSWA_CONFIGS:
            idx = t[:, None] - dil * jnp.arange(window // dil + 1)[None, :]
            idxc = jnp.maximum(idx, 0)
            kg = kf[:, :, idxc]
            s = jnp.einsum("bhqd,bhqnd->bhqn", qs, kg)
            scores.append(jnp.where(idx >= 0, s, NEG_INF))
            values.append(vf[:, :, idxc])
        m = functools.reduce(jnp.maximum, [s.max(axis=-1, keepdims=True) for s in scores])
        num, den = 0.0, 0.0
        for s, vg in zip(scores, values):
            e = jnp.exp(s - m)
            num = num + jnp.einsum("bhqn,bhqnd->bhqd", e, vg)
            den = den + e.sum(axis=-1, keepdims=True)
        return num / den

    o = lax.map(block, (jnp.arange(nb), qb))
    return o.transpose(1, 2, 0, 3, 4).reshape(B, H, S, dh).astype(q.dtype)


def dilated_mixer(p, cos, sin):
    q, k, v = [to_heads(t) for t in jnp.split(p, 3, axis=-1)]
    q = apply_partial_rope(q, cos, sin)
    k = apply_partial_rope(k, cos, sin)
    return from_heads(dilated_attention(q, k, v))


def forgetting_attention(q, k, v, c):
    B, H, S, dh = q.shape
    nb = S // Q_BLOCK
    qb = q.reshape(B, H, nb, Q_BLOCK, dh).transpose(2, 0, 1, 3, 4)
    cb = c.reshape(B, H, nb, Q_BLOCK).transpose(2, 0, 1, 3)
    kf = k.astype(jnp.float32)
    vf = v.astype(jnp.float32)
    kpos = jnp.arange(S)
    scale = dh ** -0.5

    def block(args):
        i, qi, ci = args
        t = i * Q_BLOCK + jnp.arange(Q_BLOCK)
        s = jnp.einsum("bhqd,bhkd->bhqk", qi.astype(jnp.float32) * scale, kf)
        s = s + ci[..., None] - c[:, :, None, :]
        s = jnp.where(kpos[None, :] <= t[:, None], s, NEG_INF)
        return jnp.einsum("bhqk,bhkd->bhqd", jax.nn.softmax(s, axis=-1), vf)

    o = lax.map(block, (jnp.arange(nb), qb, cb))
    return o.transpose(1, 2, 0, 3, 4).reshape(B, H, S, dh).astype(q.dtype)


def forgetting_mixer(p, b_f):
    G = GROUP_WIDTH
    q, k, v = to_heads(p[..., :G]), to_heads(p[..., G:2 * G]), to_heads(p[..., 2 * G:3 * G])
    logf = jax.nn.log_sigmoid(p[..., 3 * G:].astype(jnp.float32) + b_f.astype(jnp.float32))
    c = jnp.cumsum(logf, axis=1).transpose(0, 2, 1)
    return from_heads(forgetting_attention(q, k, v, c))


def pool_mixer(p, w_pool, scale):
    B, S, _ = p.shape
    pf = p.astype(jnp.float32).reshape(B, S, len(POOL_WINDOWS), POOL_GROUP)
    cs = jnp.cumsum(pf, axis=1)
    cs0 = jnp.concatenate([jnp.zeros_like(cs[:, :1]), cs], axis=1)
    t = jnp.arange(S)
    outs = []
    for g, w in enumerate(POOL_WINDOWS):
        upper = cs0[:, 1:, g]
        lower = cs0[:, jnp.maximum(t + 1 - w, 0), g]
        cnt = jnp.minimum(t + 1, w).astype(jnp.float32)[None, :, None]
        outs.append((upper - lower) / cnt - pf[:, :, g])
    z = jnp.stack(outs, axis=2)
    y = jnp.einsum("bsgc,gcd->bsgd", z, w_pool.astype(jnp.float32)).reshape(B, S, GROUP_WIDTH)
    return (y * scale.astype(jnp.float32)).astype(p.dtype)


def memory_cross_attention(xn, memn, w_q, w_kv, w_o):
    B, S, D = xn.shape
    M = memn.shape[1]
    q = (xn @ w_q).reshape(B, S, XA_HEADS, XA_HEAD_DIM)
    k, v = jnp.split(memn @ w_kv, 2, axis=-1)
    k = k.reshape(B, M, XA_HEADS, XA_HEAD_DIM)
    v = v.reshape(B, M, XA_HEADS, XA_HEAD_DIM)
    s = jnp.einsum("bshd,bmhd->bhsm", q.astype(jnp.float32), k.astype(jnp.float32)) * XA_HEAD_DIM ** -0.5
    o = jnp.einsum("bhsm,bmhd->bshd", jax.nn.softmax(s, axis=-1), v.astype(jnp.float32))
    return o.reshape(B, S, D).astype(xn.dtype) @ w_o


def conv_ffn(xn, w_up, w_conv, w_down):
    u = causal_depthwise_conv(xn @ w_up, w_conv)
    a, g = jnp.split(u, 2, axis=-1)
    return (a * jax.nn.silu(g)) @ w_down


def setup_inputs(seed: int = 0) -> dict:
    key = jax.random.key(seed)
    ks = jax.random.split(key, 24)
    f32 = jnp.float32
    nrm = lambda k, shape, s: jax.random.normal(k, shape, f32) * s
    L, D, G = DEPTH, D_MODEL, GROUP_WIDTH
    x = jax.random.normal(ks[0], (BATCH, SEQ, D), f32)
    mem = jax.random.normal(ks[1], (BATCH, MEM_LEN, D), f32)
    offset = jax.random.randint(ks[2], (BATCH, 1), 0, 1024, dtype=jnp.int32)
    positions = (jnp.arange(SEQ, dtype=jnp.int32)[None, :] + offset).astype(jnp.int32)
    return {
        "x": x,
        "mem": mem,
        "positions": positions,
        "g_mix": 1.0 + nrm(ks[3], (L, D), 0.02),
        "w_in": nrm(ks[4], (L, D, N_IN), D ** -0.5),
        "b_forget": FORGET_BIAS_INIT + nrm(ks[5], (L, N_GROUP_HEADS), 0.1),
        "w_sconv": nrm(ks[6], (L, SHORT_CONV_K, G), SHORT_CONV_K ** -0.5),
        "w_pool": nrm(ks[7], (L, len(POOL_WINDOWS), POOL_GROUP, POOL_GROUP), POOL_GROUP ** -0.5),
        "pool_scale": 1.0 + nrm(ks[8], (L, G), 0.02),
        "w_out": nrm(ks[9], (L, MIX_WIDTH, D), MIX_WIDTH ** -0.5),
        "g_xa": 1.0 + nrm(ks[10], (L, D), 0.02),
        "g_mem": 1.0 + nrm(ks[11], (L, D), 0.02),
        "w_xq": nrm(ks[12], (L, D, D), D ** -0.5),
        "w_xkv": nrm(ks[13], (L, D, 2 * D), D ** -0.5),
        "w_xo": nrm(ks[14], (L, D, D), D ** -0.5),
        "g_ffn": 1.0 + nrm(ks[15], (L, D), 0.02),
        "w_up": nrm(ks[16], (L, D, 2 * D_FF), D ** -0.5),
        "w_ffconv": nrm(ks[17], (L, FFN_CONV_K, 2 * D_FF), FFN_CONV_K ** -0.5),
        "w_down": nrm(ks[18], (L, D_FF, D), D_FF ** -0.5),
        "g_final": 1.0 + nrm(ks[19], (D,), 0.02),
    }


def reference(x, mem, positions, g_mix, w_in, b_forget, w_sconv, w_pool, pool_scale, w_out,
              g_xa, g_mem, w_xq, w_xkv, w_xo, g_ffn, w_up, w_ffconv, w_down, g_final):
    G = GROUP_WIDTH
    cos, sin = rope_tables(positions)
    h = x
    for l in range(DEPTH):
        xn = rms_norm(h, g_mix[l])
        proj = xn @ w_in[l]
        pa, pb, pc, pd = jnp.split(proj, [3 * G, 6 * G, 9 * G + N_GROUP_HEADS], axis=-1)
        ya = short_conv_mixer(pa, w_sconv[l])
        yb = dilated_mixer(pb, cos, sin)
        yc = forgetting_mixer(pc, b_forget[l])
        yd = pool_mixer(pd, w_pool[l], pool_scale[l])
        h = h + jnp.concatenate([ya, yb, yc, yd], axis=-1) @ w_out[l]
        h = h + memory_cross_attention(rms_norm(h, g_xa[l]), rms_norm(mem, g_mem[l]),
                                       w_xq[l], w_xkv[l], w_xo[l])
        h = h + conv_ffn(rms_norm(h, g_ffn[l]), w_up[l], w_ffconv[l], w_down[l])
    return rms_norm(h, g_final)
```

```python
import math
from contextlib import ExitStack

import numpy as np
import ml_dtypes
import concourse.bass as bass
import concourse.mybir as mybir
from concourse.bass_utils import run_bass_kernel_spmd

F32 = mybir.dt.float32
BF16 = mybir.dt.bfloat16
I32 = mybir.dt.int32
AF = mybir.ActivationFunctionType
ALU = mybir.AluOpType

ENGS = ("pe", "act", "dve", "pool", "sp")
ENGOBJ = {"pe": "tensor", "act": "scalar", "dve": "vector", "pool": "gpsimd", "sp": "sync"}

S = 4096
D = 1024
NTB = 32
NTG = 8
G = 256
NIN = 2564
DFF = 2816
MEM = 256
EPS = 1e-6
MW_OFF = 384
MW_W = 2944


class Sched:
    NDMA = 24

    def __init__(self, nc):
        self.nc = nc
        self.streams = {e: [] for e in ENGS}
        self.esem = {e: nc.alloc_semaphore(name=f"prog_{e}") for e in ENGS}
        self.ecount = {e: 0 for e in ENGS}
        self.known = {e: {} for e in ENGS}
        self.dsem = [nc.alloc_semaphore(name=f"dma_{i}") for i in range(self.NDMA)]
        self.dcount = [0] * self.NDMA
        self.drr = 0
        self.last_write = {}
        self.reads = {}
        self.nblocks = 0

    def _deps(self, eng, reads, writes, is_dma=False):
        deps = []
        skip = None if is_dma else eng
        for r in reads:
            ev = self.last_write.get(r)
            if ev is not None:
                deps.append(ev)
        for w in writes:
            ev = self.last_write.get(w)
            if ev is not None and ev[2] != skip:
                deps.append(ev)
            for ev in self.reads.get(w, ()):
                if ev[2] != skip:
                    deps.append(ev)
        best = {}
        for (sem, val, e) in deps:
            key = id(sem)
            if key not in best or best[key][1] < val:
                best[key] = (sem, val)
        out = []
        kn = self.known[eng]
        for key, (sem, val) in best.items():
            if kn.get(key, 0) >= val:
                continue
            kn[key] = val
            out.append((sem, val))
        return out

    def _record(self, ev, reads, writes):
        for r in reads:
            self.reads.setdefault(r, []).append(ev)
        for w in writes:
            self.last_write[w] = ev
            self.reads[w] = []

    def op(self, eng, fn, reads=(), writes=()):
        waits = self._deps(eng, reads, writes)
        self.ecount[eng] += 1
        ev = (self.esem[eng], self.ecount[eng], eng)
        self.streams[eng].append((waits, fn, (self.esem[eng], 1)))
        self._record(ev, reads, writes)
        return ev

    def dma(self, q, out, in_, reads=(), writes=(), **kw):
        waits = self._deps(q, reads, writes, is_dma=True)
        i = self.drr
        self.drr = (self.drr + 1) % self.NDMA
        sem = self.dsem[i]
        if self.dcount[i] > 0:
            key = id(sem)
            if self.known[q].get(key, 0) < self.dcount[i]:
                self.known[q][key] = self.dcount[i]
                waits.append((sem, self.dcount[i]))
        self.dcount[i] += 16
        ev = (sem, self.dcount[i], "dma")
        self.streams[q].append(
            (waits, lambda e, out=out, in_=in_, kw=kw: e.dma_start(out=out, in_=in_, **kw), (sem, 16))
        )
        self._record(ev, reads, writes)
        return ev

    def phase_end(self):
        waits = [(self.dsem[i], self.dcount[i]) for i in range(self.NDMA) if self.dcount[i] > 0]
        self.streams["sp"].append((waits, None, None))
        nc = self.nc
        with nc.Block() as block:
            for e in ENGS:
                stream = self.streams[e]

                def body(eng, stream=stream):
                    for waits, fn, inc in stream:
                        for sem, val in waits:
                            eng.wait_ge(sem, val)
                        if fn is not None:
                            fn(eng).then_inc(inc[0], inc[1])

                getattr(block, ENGOBJ[e])(body)
        self.streams = {e: [] for e in ENGS}
        self.last_write = {}
        self.reads = {}
        self.nblocks += 1


def bcast_rows(ap1d, nparts):
    (st, cnt), = ap1d.ap
    return bass.AP(ap1d.tensor, ap1d.offset, [[0, nparts], [st, cnt]])


class Builder:
    def __init__(self, debug=None):
        self.debug = debug or set()
        self.nc = bass.Bass("TRN2", target_bir_lowering=False)
        self.k = None
        self.dbg_outputs = []

    def sbuf(self, name, shape, dt):
        self._uid = getattr(self, "_uid", 0) + 1
        return self.nc.sbuf_tensor(f"{name}_u{self._uid}", shape, dt)

    def dram_in(self, name, shape, dt=F32):
        return self.nc.dram_tensor(name, list(shape), dt, kind="ExternalInput").ap()

    def scratch(self, name, shape, dt):
        if name in self.debug:
            self.dbg_outputs.append(name)
            return self.nc.dram_tensor(name, list(shape), dt, kind="ExternalOutput").ap()
        return self.nc.dram_tensor(name, list(shape), dt).ap()

    def build(self, n_layers=2, stop_after=None):
        nc = self.nc
        self.x = self.dram_in("x", [S, D])
        self.mem = self.dram_in("mem", [MEM, D])
        self.pos = self.dram_in("positions", [1, S], I32)
        L = 2
        self.g_mix = self.dram_in("g_mix", [L, D])
        self.w_in = self.dram_in("w_in", [L, D, NIN])
        self.b_forget = self.dram_in("b_forget", [L, 4])
        self.w_sconv = self.dram_in("w_sconv", [L, 3, G])
        self.w_pool = self.dram_in("w_pool", [L, 4, 64, 64])
        self.pool_scale = self.dram_in("pool_scale", [L, G])
        self.w_out = self.dram_in("w_out", [L, D, D])
        self.g_xa = self.dram_in("g_xa", [L, D])
        self.g_mem = self.dram_in("g_mem", [L, D])
        self.w_xq = self.dram_in("w_xq", [L, D, D])
        self.w_xkv = self.dram_in("w_xkv", [L, D, 2 * D])
        self.w_xo = self.dram_in("w_xo", [L, D, D])
        self.g_ffn = self.dram_in("g_ffn", [L, D])
        self.w_up = self.dram_in("w_up", [L, D, 2 * DFF])
        self.w_ffconv = self.dram_in("w_ffconv", [L, 3, 2 * DFF])
        self.w_down = self.dram_in("w_down", [L, DFF, D])
        self.g_final = self.dram_in("g_final", [D])
        self.c_ident = self.dram_in("c_ident", [128, 128])
        self.c_mw = self.dram_in("c_mw", [128, MW_W])
        self.c_invf = self.dram_in("c_invf", [16, 2])
        self.c_rc = self.dram_in("c_rc", [128, 2, 16])

        self.out = nc.dram_tensor("out", [S, D], F32, kind="ExternalOutput").ap()
        self.hs = self.scratch("hs", [S, D], F32)
        self.qk = self.scratch("qk", [2, 2, 4, 65, S], BF16)
        self.vs = self.scratch("vs", [2, S, G], BF16)
        self.yt = self.scratch("yt", [D, S], BF16)
        self.actt = self.scratch("actt", [DFF, S], BF16)
        self.rope = self.scratch("rope", [2, 16, S], F32)
        self.cn = self.scratch("cn", [128, 128], F32)

        self.k = Sched(nc)
        with ExitStack() as es:
            self.es_global = es
            self.ident_bf = es.enter_context(self.sbuf("ident_bf", [128, 128], BF16))
            self.ident_f = es.enter_context(self.sbuf("ident_f", [128, 128], F32))
            self.ones_bf = es.enter_context(self.sbuf("ones_bf", [128, 128], BF16))
            self.kT_sb = es.enter_context(self.sbuf("kT_sb", [128, 8, MEM], BF16))
            self.v_sb = es.enter_context(self.sbuf("v_sb", [128, 2, D], BF16))
            self.xnT = es.enter_context(self.sbuf("xnT", [128, 8, S], BF16))
            self.psb = [es.enter_context(nc.psum_tensor(f"psb{i}", [128, 512], F32)) for i in range(7)]
            self.pst = es.enter_context(nc.psum_tensor("pst", [128, 1024], BF16))
            self.phase_setup()
            done = False
            for l in range(n_layers):
                steps = [
                    ("mix_proj", lambda l=l: self.phase_mix_proj(l)),
                    ("attn", lambda l=l: self.phase_attn(l)),
                    ("mix_out", lambda l=l: self.phase_outproj(self.yt, 8, self.w_out[l], self.x if l == 0 else self.hs,
                                                               norm_g=self.g_xa[l], pieces_fn=lambda es, l=l: self.xa_kv_pieces(l, es))),
                    ("xa_kv", lambda l=l: self.phase_xa_kv(l)),
                    ("xa", lambda l=l: self.phase_xa(l)),
                    ("xa_out", lambda l=l: self.phase_outproj(self.yt, 8, self.w_xo[l], self.hs, norm_g=self.g_ffn[l])),
                    ("ffn_up", lambda l=l: self.phase_ffn_up(l)),
                    ("ffn_down", lambda l=l: self.phase_outproj(
                        self.actt, 22, self.w_down[l], self.hs,
                        norm_g=(self.g_mix[l + 1] if l + 1 < n_layers else self.g_final), final=(l + 1 == n_layers))),
                ]
                for name, fn in steps:
                    fn()
                    if stop_after == (l, name):
                        done = True
                        break
                if done:
                    break
            if not done and n_layers < 2:
                self.phase_final()
        return nc

    def phase_setup(self):
        nc, k = self.nc, self.k
        with ExitStack() as es:
            sb = lambda n, s, d: es.enter_context(self.sbuf(n, s, d))
            k.dma("sp", self.ident_f[:], self.c_ident, writes=["idf"])
            k.op("dve", lambda e: e.tensor_copy(out=self.ident_bf[:], in_=self.ident_f[:]), reads=["idf"], writes=["idb"])
            k.op("pool", lambda e: e.memset(self.ones_bf[:], 1.0), writes=["ones"])
            posi = sb("posi", [16, S], I32)
            ang = sb("ang", [16, S], F32)
            rr = sb("rr", [16, S], F32)
            ri = sb("ri", [16, S], I32)
            rif = sb("rif", [16, S], F32)
            invf = sb("invf", [16, 2], F32)
            k.dma("sp", posi[:], bass.AP(self.pos.tensor, 0, [[0, 16], [1, S]]), writes=["posi"])
            k.dma("sp", invf[:], self.c_invf, writes=["invf"])
            k.op("dve", lambda e: e.tensor_copy(out=ang[:], in_=posi[:]), reads=["posi"], writes=["ang"])
            k.op("dve", lambda e: e.tensor_scalar(out=ang[:], in0=ang[:], scalar1=invf[:, 0:1], scalar2=None, op0=ALU.mult),
                 reads=["ang", "invf"], writes=["ang"])
            inv2pi = 1.0 / (2.0 * math.pi)
            for t, shift in ((0, 0.25), (1, 0.0)):
                k.op("dve", lambda e, shift=shift: e.tensor_scalar(out=rr[:], in0=ang[:], scalar1=inv2pi, scalar2=shift,
                                                                    op0=ALU.mult, op1=ALU.add), reads=["ang"], writes=["rr"])
                k.op("dve", lambda e: e.tensor_copy(out=ri[:], in_=rr[:]), reads=["rr"], writes=["ri"])
                k.op("pool", lambda e: e.tensor_copy(out=rif[:], in_=ri[:]), reads=["ri"], writes=["rif"])
                k.op("dve", lambda e: e.tensor_tensor(out=rr[:], in0=rr[:], in1=rif[:], op=ALU.subtract),
                     reads=["rr", "rif"], writes=["rr"])
                k.op("act", lambda e: e.activation(out=rr[:], in_=rr[:], func=AF.Sin, scale=2.0 * math.pi), reads=["rr"], writes=["rr"])
                if t == 1:
                    k.op("dve", lambda e: e.tensor_scalar(out=rr[:], in0=rr[:], scalar1=invf[:, 1:2], scalar2=None, op0=ALU.mult),
                         reads=["rr", "invf"], writes=["rr"])
                k.dma("sp", self.rope[t], rr[:], reads=["rr"], writes=[("rope", t)])
            k.phase_end()

    def emit_norm(self, es, src, g_row, xnT, ntb, tag):
        nc, k = self.nc, self.k
        sb = lambda n, s, d: es.enter_context(self.sbuf(n + tag, s, d))
        gb = sb("gb", [128, D], F32)
        hb = [sb(f"hb{i}", [128, D], F32) for i in range(2)]
        junk = sb(tag + "junk", [128, D], BF16)
        ss = [sb(tag + f"ss{i}", [128, 1], F32) for i in range(2)]
        rs = [sb(tag + f"rs{i}", [128, 1], F32) for i in range(2)]
        xn = [sb(tag + f"xn{i}", [128, D], BF16) for i in range(2)]
        k.dma("sp", gb[:], bcast_rows(g_row, 128), writes=[tag + "gb"])
        hb.append(sb("hb2", [128, D], F32))
        nh = len(hb)
        psts = [self.pst[:], self.psb[0][:].bitcast(BF16)]
        pkeys = ["pst", "ps0"]

        def load(tb):
            k.dma("sp", hb[tb % nh][:], src[tb * 128:(tb + 1) * 128, :], writes=[tag + f"hb{tb % nh}"])

        def stage_a(tb):
            i = tb % 2
            hi = tb % nh
            k.op("act", lambda e: e.activation(out=junk[:], in_=hb[hi][:], func=AF.Square, accum_out=ss[i][:, 0:1]),
                 reads=[tag + f"hb{hi}"], writes=[tag + "junk", tag + f"ss{i}"])
            k.op("act", lambda e: e.activation(out=ss[i][:], in_=ss[i][:], func=AF.Sqrt, bias=EPS, scale=1.0 / D),
                 reads=[tag + f"ss{i}"], writes=[tag + f"ss{i}"])
            k.op("dve", lambda e: e.reciprocal(out=rs[i][:], in_=ss[i][:]), reads=[tag + f"ss{i}"], writes=[tag + f"rs{i}"])
            k.op("dve", lambda e: e.scalar_tensor_tensor(out=xn[i][:], in0=hb[hi][:], scalar=rs[i][:, 0:1], in1=gb[:],
                                                         op0=ALU.mult, op1=ALU.mult),
                 reads=[tag + f"hb{hi}", tag + f"rs{i}", tag + "gb"], writes=[tag + f"xn{i}"])

        def stage_b(tb):
            i = tb % 2
            pt = psts[i]
            for kc in range(8):
                k.op("pe", lambda e, kc=kc: e.transpose(pt[:, kc * 128:(kc + 1) * 128], xn[i][:, kc * 128:(kc + 1) * 128],
                                                        self.ident_bf[:]),
                     reads=[tag + f"xn{i}"], writes=[pkeys[i]])
            eng = "act" if tb % 2 == 0 else "dve"
            if eng == "act":
                k.op("act", lambda e: e.copy(out=xnT[:, :, tb * 128:(tb + 1) * 128], in_=pt.rearrange("p (k t) -> p k t", k=8)),
                     reads=[pkeys[i]], writes=[(tag + "xnT", tb)])
            else:
                k.op("dve", lambda e: e.tensor_copy(out=xnT[:, :, tb * 128:(tb + 1) * 128], in_=pt.rearrange("p (k t) -> p k t", k=8)),
                     reads=[pkeys[i]], writes=[(tag + "xnT", tb)])

        load(0)
        if ntb > 1:
            load(1)
        stage_a(0)
        for tb in range(ntb):
            if tb + 2 < ntb:
                load(tb + 2)
            if tb + 1 < ntb:
                stage_a(tb + 1)
            stage_b(tb)

    def wload(self, dst, w2d, c0, n, key, nkc=8):
        src = w2d.rearrange("(kc p) n -> p kc n", p=128)[:, :, c0:c0 + n]
        self.k.dma("pool", dst[:, 0:nkc, 0:n], src, writes=[key])

    def gemm_fm(self, ps_ap, wt, m, xnT, t0, n, wkey, pkey, xkeys=()):
        for kc in range(8):
            self.k.op("pe", lambda e, kc=kc: e.matmul(ps_ap, lhsT=wt[:, kc, 0:m], rhs=xnT[:, kc, t0:t0 + n],
                                                      start=(kc == 0), stop=(kc == 7)),
                      reads=[wkey] + list(xkeys), writes=[pkey])

    def phase_mix_proj(self, l):
        nc, k = self.nc, self.k
        src = self.x if l == 0 else self.hs
        w = self.w_in[l]
        with ExitStack() as es0:
            xnT = self.xnT
            if l == 0:
                with ExitStack() as es:
                    self.emit_norm(es, src, self.g_mix[l], xnT, NTB, "m")
                    k.phase_end()
            with ExitStack() as es:
                sb = lambda n, s, d: es.enter_context(self.sbuf(n, s, d))
                PADW = 16
                stA = sb("stA", [128, PADW + S], F32)
                stB = sb("stB", [128, PADW + S], F32)
                stC = sb("stC", [128, PADW + S], F32)
                yo = [sb(f"yo{i}", [128, S], BF16) for i in range(2)]
                wt = [sb(f"wt{i}", [128, 8, 128], BF16) for i in range(3)]
                wc = sb("wc", [128, 3, 2], F32)
                wp = [sb(f"wp{i}", [128, 128], BF16) for i in range(2)]
                psc = sb("psc", [128, 2], F32)
                rc = sb("rc", [128, 2, 16], F32)
                tmp16 = sb("tmp16", [128, 16], F32)
                for t_, nm in ((stA, "stA"), (stB, "stB"), (stC, "stC")):
                    k.op("pool", lambda e, t_=t_: e.memset(t_[:, 0:PADW], 0.0), writes=[nm])
                for t_ in range(3):
                    k.dma("sp", wc[:, t_, :], self.w_sconv[l, t_].rearrange("(c p) -> p c", p=128), writes=["wc"],
                          allow_slow_non_contiguous=True)
                k.dma("sp", psc[:], self.pool_scale[l].rearrange("(c p) -> p c", p=128), writes=["psc"],
                      allow_slow_non_contiguous=True)
                k.dma("sp", rc[:], self.c_rc, writes=["rc"])
                nps = [0]

                def proj_to(dst_fn, c0, wi):
                    self.wload(wt[wi], w, c0, 128, f"wt{wi}")
                    for tg in range(NTG):
                        b = nps[0] % 6
                        nps[0] += 1
                        self.gemm_fm(self.psb[b][:, :], wt[wi], 128, xnT, tg * 512, 512, f"wt{wi}", f"ps{b}")
                        dst_fn(tg, self.psb[b], f"ps{b}")

                for c in range(2):
                    def ev_c(tg, ps, pk):
                        k.op("act", lambda e: e.copy(out=stA[:, PADW + tg * 512:PADW + (tg + 1) * 512], in_=ps[:, :]),
                             reads=[pk], writes=["stA"])
                    proj_to(ev_c, 512 + 128 * c, 0)

                    def ev_h(tg, ps, pk):
                        sl = slice(PADW + tg * 512, PADW + (tg + 1) * 512)
                        k.op("dve", lambda e: e.tensor_tensor(out=stA[:, sl], in0=ps[:, :], in1=stA[:, sl], op=ALU.mult),
                             reads=[pk, "stA"], writes=["stA"])
                    proj_to(ev_h, 0 + 128 * c, 1)
                    k.op("dve", lambda e, c=c: e.tensor_scalar(out=stB[:, PADW:], in0=stA[:, PADW - 2:PADW - 2 + S], scalar1=wc[:, 0, c:c + 1],
                                                                scalar2=None, op0=ALU.mult), reads=["stA", "wc"], writes=["stB"])
                    k.op("dve", lambda e, c=c: e.scalar_tensor_tensor(out=stB[:, PADW:], in0=stA[:, PADW - 1:PADW - 1 + S],
                                                                       scalar=wc[:, 1, c:c + 1], in1=stB[:, PADW:], op0=ALU.mult, op1=ALU.add),
                         reads=["stA", "wc", "stB"], writes=["stB"])
                    k.op("dve", lambda e, c=c: e.scalar_tensor_tensor(out=stB[:, PADW:], in0=stA[:, PADW:PADW + S],
                                                                       scalar=wc[:, 2, c:c + 1], in1=stB[:, PADW:], op0=ALU.mult, op1=ALU.add),
                         reads=["stA", "wc", "stB"], writes=["stB"])

                    def ev_b(tg, ps, pk, c=c):
                        k.op("dve", lambda e: e.tensor_tensor(out=yo[c][:, tg * 512:(tg + 1) * 512], in0=ps[:, :],
                                                              in1=stB[:, PADW + tg * 512:PADW + (tg + 1) * 512], op=ALU.mult),
                             reads=[pk, "stB"], writes=[f"yo{c}"])
                    proj_to(ev_b, 256 + 128 * c, 2)
                    k.dma("sp", self.yt[128 * c:128 * (c + 1), :], yo[c][:], reads=[f"yo{c}"], writes=[("yt", c)])

                for c in range(2):
                    k.op("pool", lambda e, c=c: e.memset(wp[c][:], 0.0), writes=[f"wp{c}"])
                    k.dma("pool", wp[c][0:64, 0:64], self.w_pool[l, 2 * c], writes=[f"wp{c}"])
                    k.dma("pool", wp[c][64:128, 64:128], self.w_pool[l, 2 * c + 1], writes=[f"wp{c}"])

                    def ev_p(tg, ps, pk):
                        k.op("act", lambda e: e.copy(out=stA[:, PADW + tg * 512:PADW + (tg + 1) * 512], in_=ps[:, :]),
                             reads=[pk], writes=["stA"])
                    proj_to(ev_p, 2308 + 128 * c, c)
                    chain = [(stA, "stA", stB, "stB", 1), (stB, "stB", stC, "stC", 2)]
                    if c == 1:
                        chain += [(stC, "stC", stB, "stB", 4), (stB, "stB", stC, "stC", 8)]
                    for (a, an, b_, bn, sh) in chain:
                        k.op("dve", lambda e, a=a, b_=b_, sh=sh: e.tensor_tensor(out=b_[:, PADW:], in0=a[:, PADW:], in1=a[:, PADW - sh:PADW - sh + S],
                                                                                  op=ALU.add), reads=[an], writes=[bn])
                    halves = [((0, 64), stB, "stB", 2), ((64, 128), stC, "stC", 4)] if c == 0 else \
                             [((0, 64), stB, "stB", 8), ((64, 128), stC, "stC", 16)]
                    z = yo[c]
                    for (p0, p1), sw_, swn, wlen in halves:
                        k.op("dve", lambda e, p0=p0, p1=p1, sw_=sw_, wlen=wlen: e.scalar_tensor_tensor(
                            out=z[p0:p1, :], in0=sw_[p0:p1, PADW:], scalar=1.0 / wlen, in1=stA[p0:p1, PADW:], op0=ALU.mult, op1=ALU.subtract),
                            reads=[swn, "stA"], writes=[f"yo{c}"])
                        k.op("dve", lambda e, p0=p0, p1=p1, sw_=sw_, c=c: e.tensor_tensor(
                            out=tmp16[p0:p1, :], in0=sw_[p0:p1, PADW:PADW + 16], in1=rc[p0:p1, c, :], op=ALU.mult),
                            reads=[swn, "rc"], writes=["tmp16"])
                        k.op("dve", lambda e, p0=p0, p1=p1: e.tensor_tensor(
                            out=z[p0:p1, 0:16], in0=tmp16[p0:p1, :], in1=stA[p0:p1, PADW:PADW + 16], op=ALU.subtract),
                            reads=["tmp16", "stA", f"yo{c}"], writes=[f"yo{c}"])
                    yo2 = stB[:].bitcast(BF16)
                    for tg in range(NTG):
                        b = nps[0] % 6
                        nps[0] += 1
                        k.op("pe", lambda e, b=b, c=c, tg=tg: e.matmul(self.psb[b][:, :], lhsT=wp[c][:, :], rhs=z[:, tg * 512:(tg + 1) * 512],
                                                                       start=True, stop=True), reads=[f"wp{c}", f"yo{c}"], writes=[f"ps{b}"])
                        k.op("act", lambda e, b=b, c=c, tg=tg: e.activation(out=yo2[:, 2 * PADW + tg * 512:2 * PADW + (tg + 1) * 512], in_=self.psb[b][:, :], func=AF.Copy,
                                                                            scale=psc[:, c:c + 1]), reads=[f"ps{b}", "psc", "stB"], writes=["stB"])
                    k.dma("sp", self.yt[768 + 128 * c:768 + 128 * (c + 1), :], yo2[:, 2 * PADW:2 * PADW + S], reads=["stB"], writes=[("yt", 6 + c)])
                k.phase_end()
            with ExitStack() as es:
                sb = lambda n, s, d: es.enter_context(self.sbuf(n, s, d))
                wt = [sb(f"wq{i}", [128, 8, 128], BF16) for i in range(3)]
                wv = sb("wv", [128, 8, 512], BF16)
                NR = 6
                st16 = [sb(f"st16_{i}", [16, 512], F32) for i in range(NR)]
                swp = [sb(f"swp{i}", [16, 512], F32) for i in range(NR)]
                t1 = [sb(f"t1_{i}", [16, 512], F32) for i in range(NR)]
                cs = sb("cs", [16, S], F32)
                sn = sb("sn", [16, S], F32)
                k.dma("sp", cs[:], self.rope[0], writes=["cs"])
                k.dma("sp", sn[:], self.rope[1], writes=["sn"])
                qst = [sb(f"qst{i}", [64, S], BF16) for i in range(4)]
                vst = [sb(f"vst{i}", [128, 512], BF16) for i in range(2)]
                nps = 0
                nrope = 0
                ncs = 0
                nq = 0
                for typ in range(2):
                    for qk_ in range(2):
                        for hp in range(2):
                            c0 = 768 + typ * 768 + qk_ * 256 + 128 * hp
                            wi = nq % 3
                            qa = 2 * (nq % 2)
                            nq += 1
                            self.wload(wt[wi], w, c0, 128, f"wq{wi}")
                            scale = 0.125 if qk_ == 0 else 1.0
                            for tg in range(NTG):
                                b = nps % 6
                                nps += 1
                                self.gemm_fm(self.psb[b][:, :], wt[wi], 128, xnT, tg * 512, 512, f"wq{wi}", f"ps{b}")
                                for hh in range(2):
                                    qi = qa + hh
                                    p0 = 64 * hh
                                    k.op("act", lambda e, b=b, qi=qi, tg=tg, scale=scale, p0=p0: e.activation(
                                        out=qst[qi][:, tg * 512:(tg + 1) * 512], in_=self.psb[b][p0:p0 + 64, :], func=AF.Copy, scale=scale),
                                        reads=[f"ps{b}"], writes=[(f"qst{qi}", tg)])
                                    if typ == 1:
                                        continue
                                    r = nrope % NR
                                    nrope += 1
                                    k.op("act", lambda e, b=b, r=r, scale=scale, p0=p0: e.activation(
                                        out=st16[r][:], in_=self.psb[b][p0:p0 + 16, :], func=AF.Copy, scale=scale),
                                        reads=[f"ps{b}"], writes=[f"st16_{r}"])
                                    k.dma("sp", swp[r][0:8, :], st16[r][8:16, :], reads=[f"st16_{r}"], writes=[f"swp{r}a"])
                                    k.dma("sp", swp[r][8:16, :], st16[r][0:8, :], reads=[f"st16_{r}"], writes=[f"swp{r}b"])
                                    k.op("dve", lambda e, r=r, tg=tg: e.tensor_tensor(out=t1[r][:], in0=swp[r][:], in1=sn[:, tg * 512:(tg + 1) * 512], op=ALU.mult),
                                         reads=[f"swp{r}a", f"swp{r}b", "sn"], writes=[f"t1_{r}"])
                                    k.op("dve", lambda e, r=r, tg=tg: e.tensor_tensor(out=st16[r][:], in0=st16[r][:], in1=cs[:, tg * 512:(tg + 1) * 512], op=ALU.mult),
                                         reads=[f"st16_{r}", "cs", f"swp{r}a", f"swp{r}b"], writes=[f"st16_{r}"])
                                    k.op("dve", lambda e, r=r, qi=qi, tg=tg: e.tensor_tensor(
                                        out=qst[qi][0:16, tg * 512:(tg + 1) * 512], in0=st16[r][:], in1=t1[r][:], op=ALU.add),
                                        reads=[f"st16_{r}", f"t1_{r}", (f"qst{qi}", tg)], writes=[(f"qst{qi}", tg)])
                            for hh in range(2):
                                qi = qa + hh
                                k.dma("sp", self.qk[typ, qk_, 2 * hp + hh, 0:64, :], qst[qi][:], reads=[(f"qst{qi}", tg) for tg in range(NTG)],
                                      writes=[("qk", typ, qk_, hp, hh)])
                src_v = w.rearrange("(kc p) n -> p kc n", p=128)
                k.dma("pool", wv[:, :, 0:256], src_v[:, :, 1280:1536], writes=["wv"])
                k.dma("pool", wv[:, :, 256:512], src_v[:, :, 2048:2304], writes=["wv"])
                for tb in range(NTB):
                    b = nps % 6
                    nps += 1
                    i = tb % 2
                    for kc in range(8):
                        k.op("pe", lambda e, b=b, kc=kc, tb=tb: e.matmul(self.psb[b][:, :], lhsT=xnT[:, kc, tb * 128:(tb + 1) * 128], rhs=wv[:, kc, :],
                                                                         start=(kc == 0), stop=(kc == 7)), reads=["wv"], writes=[f"ps{b}"])
                    k.op("act", lambda e, b=b, i=i: e.copy(out=vst[i][:], in_=self.psb[b][:, :]), reads=[f"ps{b}"], writes=[f"vst{i}"])
                    k.dma("sp", self.vs[0, tb * 128:(tb + 1) * 128, :], vst[i][:, 0:256], reads=[f"vst{i}"], writes=[("vs0", tb)])
                    k.dma("sp", self.vs[1, tb * 128:(tb + 1) * 128, :], vst[i][:, 256:512], reads=[f"vst{i}"], writes=[("vs1", tb)])
                k.phase_end()
            with ExitStack() as es:
                sb = lambda n, s, d: es.enter_context(self.sbuf(n, s, d))
                wf = sb("wf", [128, 8, 4], BF16)
                fl = sb("fl", [4, S], F32)
                cc = sb("cc", [4, S], F32)
                onesf = sb("onesf", [4, S], F32)
                crow = sb("crow", [4, S], BF16)
                negb = sb("negb", [4, 1], F32)
                cnT = sb("cnT", [128, 128], F32)
                nps = 0
                self.wload(wf, w, 2304, 4, "wf")
                k.dma("sp", negb[:], self.b_forget[l].rearrange("(p o) -> p o", o=1), writes=["negb"], allow_slow_non_contiguous=True)
                k.op("dve", lambda e: e.tensor_scalar(out=negb[:], in0=negb[:], scalar1=-1.0, scalar2=None, op0=ALU.mult),
                     reads=["negb"], writes=["negb"])
                k.op("pool", lambda e: e.memset(onesf[:], 1.0), writes=["onesf"])
                for tg in range(NTG):
                    b = nps % 6
                    nps += 1
                    self.gemm_fm(self.psb[b][0:4, :], wf, 4, xnT, tg * 512, 512, "wf", f"ps{b}")
                    k.op("act", lambda e, b=b, tg=tg: e.activation(out=fl[:, tg * 512:(tg + 1) * 512], in_=self.psb[b][0:4, :], func=AF.Exp,
                                                                   scale=-1.0, bias=negb[:, 0:1]), reads=[f"ps{b}", "negb"], writes=["fl"])
                k.op("act", lambda e: e.activation(out=fl[:], in_=fl[:], func=AF.Ln, bias=1.0), reads=["fl"], writes=["fl"])
                k.op("dve", lambda e: e.tensor_scalar(out=fl[:], in0=fl[:], scalar1=-1.0, scalar2=None, op0=ALU.mult), reads=["fl"], writes=["fl"])
                k.op("dve", lambda e: e.tensor_tensor_scan(out=cc[:], data0=onesf[:], data1=fl[:], initial=0.0, op0=ALU.mult, op1=ALU.add),
                     reads=["fl", "onesf"], writes=["cc"])
                k.op("dve", lambda e: e.tensor_copy(out=crow[:], in_=cc[:]), reads=["cc"], writes=["crow"])
                for h in range(4):
                    k.dma("sp", self.qk[1, 0, h, 64:65, :], crow[h:h + 1, :], reads=["crow"], writes=[("qkc", h)])
                k.op("dve", lambda e: e.tensor_copy(out=crow[:], in_=onesf[:]), reads=["onesf", "crow"], writes=["crow"])
                for h in range(4):
                    k.dma("sp", self.qk[1, 1, h, 64:65, :], crow[h:h + 1, :], reads=["crow"], writes=[("qk1", h)])
                pcn = self.psb[6]
                for j in range(NTB):
                    k.op("pe", lambda e, j=j: e.transpose(pcn[:, 4 * j:4 * j + 4], cc[0:4, j * 128:(j + 1) * 128], self.ident_f[0:4, 0:4]),
                         reads=["cc"], writes=["pcn"])
                k.op("dve", lambda e: e.tensor_scalar(out=cnT[:], in0=pcn[:, 0:128], scalar1=-1.0, scalar2=None, op0=ALU.mult),
                     reads=["pcn"], writes=["cnT"])
                k.dma("sp", self.cn, cnT[:], reads=["cnT"], writes=["cn"])
                k.phase_end()

    def phase_attn(self, l):
        nc, k = self.nc, self.k
        with ExitStack() as es:
            sb = lambda n, s, d: es.enter_context(self.sbuf(n, s, d))
            qT = [sb(f"qT{i}", [65, S], BF16) for i in range(2)]
            kT = [sb(f"kT{i}", [65, S], BF16) for i in range(2)]
            vt = [sb(f"vt{i}", [128, NTB, 128], BF16) for i in range(2)]
            oT = [sb(f"oT{i}", [64, S], BF16) for i in range(2)]
            pT = [sb(f"pT{i}", [128, 512], BF16) for i in range(6)]
            rden = [sb(f"rden{i}", [128, 512], F32) for i in range(2)]
            mw = sb("mw", [128, MW_W], BF16)
            cneg = sb("cneg", [128, 128], F32)
            k.dma("pool", mw[:], self.c_mw, writes=["mw"])
            k.dma("sp", cneg[:], self.cn, writes=["cneg"])
            for i in range(2):
                k.op("pool", lambda e, i=i: e.memset(vt[i][:, :, 64:128], 1.0), writes=[f"vt{i}"])
                k.op("pool", lambda e, i=i: e.memset(qT[i][64:65, :], 0.0), writes=[f"qT{i}"])
                k.op("pool", lambda e, i=i: e.memset(kT[i][64:65, :], 0.0), writes=[f"kT{i}"])
            insts = [(t, h) for t in range(2) for h in range(4)]

            def load(n):
                t, h = insts[n]
                i = n % 2
                kr = 65 if t == 1 else 64
                k.dma("sp", qT[i][0:kr, :], self.qk[t, 0, h, 0:kr, :], writes=[f"qT{i}"])
                k.dma("sp", kT[i][0:kr, :], self.qk[t, 1, h, 0:kr, :], writes=[f"kT{i}"])
                k.dma("sp", vt[i][:, :, 0:64], self.vs[t].rearrange("(j p) d -> p j d", p=128)[:, :, 64 * h:64 * (h + 1)], writes=[f"vt{i}"])

            SB = [0, 1, 2, 3, 6]
            LA = 4
            units = []
            nacc = 0
            for n, (t, h) in enumerate(insts):
                for I in range(NTG):
                    jlo = 0 if t == 1 else max(0, 4 * I - 16)
                    a_ = nacc % 2
                    nacc += 1
                    js = list(range(jlo, 4 * I + 4))
                    for idx, j in enumerate(js):
                        c1 = 512
                        if t == 0 and 4 * I - 16 >= 0 and j - (4 * I - 16) < 3 and idx > 0:
                            c1 = 128 * (j - (4 * I - 16) + 1)
                        units.append(dict(n=n, t=t, h=h, I=I, j=j, idx=idx, last=(idx == len(js) - 1), a=a_, c1=c1,
                                          first_of_inst=(I == 0 and idx == 0), last_of_inst=(I == NTG - 1 and idx == len(js) - 1)))

            pending = []

            def emit_s(u, m):
                i = u["n"] % 2
                kr = 65
                I, j = u["I"], u["j"]
                r = j - 4 * I
                c0 = 128 * r if r > 0 else 0
                sbk = SB[m % len(SB)]
                ps_s = self.psb[sbk]
                c1 = u["c1"]
                k.op("pe", lambda e: e.matmul(ps_s[:, c0:c1], lhsT=kT[i][0:kr, j * 128:(j + 1) * 128],
                                              rhs=qT[i][0:kr, I * 512 + c0:I * 512 + c1], start=True, stop=True),
                     reads=[f"kT{i}", f"qT{i}"], writes=[f"ps{sbk}"])

            def emit_rest(u, m):
                i = u["n"] % 2
                t, h, I, j, idx, a_ = u["t"], u["h"], u["I"], u["j"], u["idx"], u["a"]
                r = j - 4 * I
                c0 = 128 * r if r > 0 else 0
                sbk = SB[m % len(SB)]
                ps_s = self.psb[sbk]
                p = m % len(pT)
                c1 = u["c1"]
                oacc = self.psb[4 + a_]
                if u["first_of_inst"] and u["n"] + 1 < len(insts):
                    load(u["n"] + 1)
                if t == 1:
                    k.op("act", lambda e: e.activation(out=pT[p][:, c0:512], in_=ps_s[:, c0:512], func=AF.Exp,
                                                       bias=cneg[:, 4 * j + h:4 * j + h + 1], scale=1.0),
                         reads=[f"ps{sbk}", "cneg"], writes=[f"pT{p}"])
                    if r >= 0:
                        k.op("pool", lambda e: e.affine_select(out=pT[p][:, c0:c0 + 128], in_=pT[p][:, c0:c0 + 128], pattern=[[1, 128]],
                                                               compare_op=ALU.is_ge, fill=0.0, base=0, channel_multiplier=-1),
                             reads=[f"pT{p}"], writes=[f"pT{p}"])
                else:
                    k.op("act", lambda e: e.activation(out=pT[p][:, c0:c1], in_=ps_s[:, c0:c1], func=AF.Exp),
                         reads=[f"ps{sbk}"], writes=[f"pT{p}"])
                    m0 = 512 * I - 128 * j + MW_OFF + c0
                    meng = "pool" if m % 4 == 3 else "dve"
                    k.op(meng, lambda e: e.tensor_tensor(out=pT[p][:, c0:c1], in0=pT[p][:, c0:c1], in1=mw[:, m0:m0 + c1 - c0], op=ALU.mult),
                         reads=[f"pT{p}", "mw"], writes=[f"pT{p}"])
                k.op("pe", lambda e: e.matmul(oacc[:, c0:c1], lhsT=vt[i][:, j, :], rhs=pT[p][:, c0:c1], start=(idx == 0), stop=u["last"]),
                     reads=[f"vt{i}", f"pT{p}"], writes=[f"po{a_}"])
                if u["last"]:
                    if t == 0:
                        k.op("act", lambda e: e.activation(out=rden[a_][64:128, :], in_=oacc[64:128, :], func=AF.Ln),
                             reads=[f"po{a_}"], writes=[f"rden{a_}"])
                        k.op("act", lambda e: e.activation(out=rden[a_][64:128, :], in_=rden[a_][64:128, :], func=AF.Exp, scale=-1.0),
                             reads=[f"rden{a_}"], writes=[f"rden{a_}"])
                    else:
                        k.op("dve", lambda e: e.reciprocal(out=rden[a_][64:128, :], in_=oacc[64:128, :]), reads=[f"po{a_}"], writes=[f"rden{a_}"])
                    k.op("dve", lambda e: e.tensor_tensor(out=oT[i][:, I * 512:(I + 1) * 512], in0=oacc[0:64, :], in1=rden[a_][64:128, :], op=ALU.mult),
                         reads=[f"po{a_}", f"rden{a_}"], writes=[f"oT{i}"])
                if u["last_of_inst"]:
                    row0 = 256 + 256 * t + 64 * h
                    k.dma("sp", self.yt[row0:row0 + 64, :], oT[i][:], reads=[f"oT{i}"], writes=[("yt", "a", t, h)])

            load(0)
            nu = len(units)
            for m in range(nu + LA):
                if m < nu:
                    emit_s(units[m], m)
                if m - LA >= 0:
                    emit_rest(units[m - LA], m - LA)
            k.phase_end()

    def phase_outproj(self, srcT, nkc, w2d, h_src, norm_g=None, final=False, pieces_fn=None):
        nc, k = self.nc, self.k
        xnT = self.xnT
        with ExitStack() as es:
            sb = lambda n, s, d: es.enter_context(self.sbuf(n, s, d))
            wo = sb("wo", [128, nkc, D], BF16)
            yt = [sb(f"ytb{i}", [128, nkc, 512], BF16) for i in range(2)]
            NHB = 4
            hb = [sb(f"hbo{i}", [128, D], F32) for i in range(NHB)]
            gb = sb("gbo", [128, D], F32)
            junk = sb("junko", [128, D], BF16)
            ss = [sb(f"sso{i}", [128, 1], F32) for i in range(2)]
            rs = [sb(f"rso{i}", [128, 1], F32) for i in range(2)]
            if final:
                ob = [sb(f"obo{i}", [128, D], F32) for i in range(2)]
            else:
                xn = [sb(f"xno{i}", [128, D], BF16) for i in range(2)]
            k.dma("sp", gb[:], bcast_rows(norm_g, 128), writes=["gb"])
            wsrc = w2d.rearrange("(kc p) n -> p kc n", p=128)
            step = 4 if nkc <= 8 else 2
            for c in range(0, nkc, step):
                k.dma("pool", wo[:, c:c + step, :], wsrc[:, c:c + step, :], writes=[("wo", c)])
            wkeys = [("wo", c) for c in range(0, nkc, step)]
            ysrc = srcT.rearrange("(kc p) t -> p kc t", p=128)
            k.dma("sp", yt[0][:], ysrc[:, :, 0:512], writes=["ytb0"])

            def loadh(tb):
                k.dma("sp", hb[tb % NHB][:], h_src[tb * 128:(tb + 1) * 128, :], writes=[f"hbo{tb % NHB}"])

            def norm_a1(tb):
                i = tb % 2
                hi = tb % NHB
                k.op("act", lambda e: e.activation(out=junk[:], in_=hb[hi][:], func=AF.Square, accum_out=ss[i][:, 0:1]),
                     reads=[f"hbo{hi}"], writes=["junk", f"ss{i}"])
                k.op("act", lambda e: e.activation(out=ss[i][:], in_=ss[i][:], func=AF.Sqrt, bias=EPS, scale=1.0 / D),
                     reads=[f"ss{i}"], writes=[f"ss{i}"])

            def norm_a2(tb):
                i = tb % 2
                hi = tb % NHB
                k.op("dve", lambda e: e.reciprocal(out=rs[i][:], in_=ss[i][:]), reads=[f"ss{i}"], writes=[f"rs{i}"])
                if final:
                    k.op("dve", lambda e: e.scalar_tensor_tensor(out=ob[i][:], in0=hb[hi][:], scalar=rs[i][:, 0:1], in1=gb[:],
                                                                 op0=ALU.mult, op1=ALU.mult),
                         reads=[f"hbo{hi}", f"rs{i}", "gb"], writes=[f"ob{i}"])
                    k.dma("act", self.out[tb * 128:(tb + 1) * 128, :], ob[i][:], reads=[f"ob{i}"], writes=[("out", tb)])
                else:
                    k.op("dve", lambda e: e.scalar_tensor_tensor(out=xn[i][:], in0=hb[hi][:], scalar=rs[i][:, 0:1], in1=gb[:],
                                                                 op0=ALU.mult, op1=ALU.mult),
                         reads=[f"hbo{hi}", f"rs{i}", "gb"], writes=[f"xn{i}"])

            def norm_b(tb):
                if final:
                    return
                i = tb % 2
                for kc in range(8):
                    k.op("pe", lambda e, kc=kc: e.transpose(self.pst[:, kc * 128:(kc + 1) * 128], xn[i][:, kc * 128:(kc + 1) * 128],
                                                            self.ident_bf[:]),
                         reads=[f"xn{i}"], writes=["pst"])
                k.op("act", lambda e: e.copy(out=xnT[:, :, tb * 128:(tb + 1) * 128], in_=self.pst[:].rearrange("p (k t) -> p k t", k=8)),
                     reads=["pst"], writes=[("xnT", tb)])

            loadh(0)
            loadh(1)
            pieces = pieces_fn(es) if pieces_fn is not None else []
            nb = 0
            for tg in range(NTG):
                yi = tg % 2
                if tg + 1 < NTG:
                    k.dma("sp", yt[1 - yi][:], ysrc[:, :, (tg + 1) * 512:(tg + 2) * 512], writes=[f"ytb{1 - yi}"])
                for tbi in range(4):
                    tb = 4 * tg + tbi
                    hi = tb % NHB
                    if tb + 2 < NTB:
                        loadh(tb + 2)
                    if pieces and tb >= 2 and tb % 2 == 0:
                        pieces.pop(0)()
                    for half in range(2):
                        b = nb % 7
                        nb += 1
                        for kc in range(nkc):
                            k.op("pe", lambda e, b=b, yi=yi, kc=kc, tbi=tbi, half=half: e.matmul(
                                self.psb[b][:, :], lhsT=yt[yi][:, kc, tbi * 128:(tbi + 1) * 128], rhs=wo[:, kc, half * 512:(half + 1) * 512],
                                start=(kc == 0), stop=(kc == nkc - 1)), reads=[f"ytb{yi}"] + (wkeys if kc == 0 else []), writes=[f"ps{b}"])
                        k.op("dve", lambda e, b=b, hi=hi, half=half: e.tensor_tensor(
                            out=hb[hi][:, half * 512:(half + 1) * 512], in0=self.psb[b][:, :], in1=hb[hi][:, half * 512:(half + 1) * 512], op=ALU.add),
                            reads=[f"ps{b}", f"hbo{hi}"], writes=[f"hbo{hi}"])
                    if not final:
                        k.dma("act", self.hs[tb * 128:(tb + 1) * 128, :], hb[hi][:], reads=[f"hbo{hi}"], writes=[("hs", tb)])
                    norm_a1(tb)
                    if tb >= 1:
                        norm_a2(tb - 1)
                    if tb >= 2:
                        norm_b(tb - 2)
            norm_a2(NTB - 1)
            norm_b(NTB - 2)
            norm_b(NTB - 1)
            while pieces:
                pieces.pop(0)()
            k.phase_end()

    def xa_kv_pieces(self, l, es):
        nc, k = self.nc, self.k
        sb = lambda n, s, d: es.enter_context(self.sbuf(n, s, d))
        mT = sb("mT", [128, 8, MEM], BF16)
        wk = [sb(f"wk{i}", [128, 8, 128], BF16) for i in range(2)]
        wv = sb("wvx", [128, 8, D], BF16)
        w = self.w_xkv[l]
        xk = [("kxnT", 0), ("kxnT", 1)]
        pieces = []
        pieces.append(lambda: self.emit_norm(es, self.mem, self.g_mem[l], mT, 2, "k"))

        def kpiece(fc):
            i = fc % 2
            self.wload(wk[i], w, fc * 128, 128, f"wk{i}")
            b = fc % 4
            self.gemm_fm(self.psb[b][:, 0:MEM], wk[i], 128, mT, 0, MEM, f"wk{i}", f"ps{b}", xkeys=xk)
            k.op("act", lambda e: e.copy(out=self.kT_sb[:, fc, :], in_=self.psb[b][:, 0:MEM]), reads=[f"ps{b}"], writes=["kT_sb"])

        for fc in range(8):
            pieces.append(lambda fc=fc: kpiece(fc))

        def vload():
            wsrc = w.rearrange("(kc p) n -> p kc n", p=128)
            k.dma("pool", wv[:, :, 0:512], wsrc[:, :, D:D + 512], writes=["wvx0"])
            k.dma("pool", wv[:, :, 512:1024], wsrc[:, :, D + 512:2 * D], writes=["wvx1"])

        pieces.append(vload)

        def vpiece(mb, half):
            b = 4 + (2 * mb + half) % 2
            for kc in range(8):
                k.op("pe", lambda e, kc=kc: e.matmul(self.psb[b][:, :], lhsT=mT[:, kc, mb * 128:(mb + 1) * 128],
                                                     rhs=wv[:, kc, half * 512:(half + 1) * 512], start=(kc == 0), stop=(kc == 7)),
                     reads=[f"wvx{half}"] + xk, writes=[f"ps{b}"])
            k.op("act", lambda e: e.copy(out=self.v_sb[:, mb, half * 512:(half + 1) * 512], in_=self.psb[b][:, :]),
                 reads=[f"ps{b}"], writes=["v_sb"])

        for mb in range(2):
            for half in range(2):
                pieces.append(lambda mb=mb, half=half: vpiece(mb, half))
        return pieces

    def phase_xa_kv(self, l):
        pass

    def phase_xa(self, l):
        nc, k = self.nc, self.k
        with ExitStack() as es0:
            xnT = self.xnT
            with ExitStack() as es:
                sb = lambda n, s, d: es.enter_context(self.sbuf(n, s, d))
                wq = [sb(f"wxq{i}", [128, 8, 128], BF16) for i in range(2)]
                qTh = [sb(f"qTh{i}", [128, 2, S], BF16) for i in range(2)]
                xo = [sb(f"xo{i}", [128, 2, S], BF16) for i in range(2)]
                pT = [sb(f"pTx{i}", [128, 512], BF16) for i in range(4)]
                rden = [sb(f"rdx{i}", [128, 512], F32) for i in range(2)]
                w = self.w_xq[l]
                nps = 0
                npt = 0
                nr = 0
                nbo = 0
                for hd in range(4):
                    qi = hd % 2
                    for dc in range(2):
                        fc = 2 * hd + dc
                        self.wload(wq[dc], w, fc * 128, 128, f"wxq{dc}")
                        for tg in range(NTG):
                            b = nps % 4
                            nps += 1
                            self.gemm_fm(self.psb[b][:, :], wq[dc], 128, xnT, tg * 512, 512, f"wxq{dc}", f"ps{b}")
                            k.op("act", lambda e, b=b, qi=qi, dc=dc, tg=tg: e.activation(
                                out=qTh[qi][:, dc, tg * 512:(tg + 1) * 512], in_=self.psb[b][:, :], func=AF.Copy, scale=1.0 / 16.0),
                                reads=[f"ps{b}"], writes=[(f"qTh{qi}", dc)])
                    ptss = {}

                    def xa_s(tg, hd=hd, qi=qi):
                        nonlocal nps, npt
                        pts = []
                        for mb in range(2):
                            b = nps % 4
                            nps += 1
                            p = npt % 4
                            npt += 1
                            pts.append(p)
                            for dc in range(2):
                                k.op("pe", lambda e, b=b, dc=dc, mb=mb: e.matmul(
                                    self.psb[b][:, :], lhsT=self.kT_sb[:, 2 * hd + dc, mb * 128:(mb + 1) * 128],
                                    rhs=qTh[qi][:, dc, tg * 512:(tg + 1) * 512], start=(dc == 0), stop=(dc == 1)),
                                    reads=[(f"qTh{qi}", 0), (f"qTh{qi}", 1)], writes=[f"ps{b}"])
                            k.op("act", lambda e, b=b, p=p: e.activation(out=pT[p][:], in_=self.psb[b][:, :], func=AF.Exp),
                                 reads=[f"ps{b}"], writes=[f"pTx{p}"])
                        ptss[tg] = pts

                    def xa_rest(tg, hd=hd, qi=qi):
                        nonlocal nr, nbo
                        pts = ptss[tg]
                        pden = self.psb[6]
                        for mb in range(2):
                            k.op("pe", lambda e, mb=mb, p=pts[mb]: e.matmul(pden[:, :], lhsT=self.ones_bf[:, :], rhs=pT[p][:],
                                                                            start=(mb == 0), stop=(mb == 1)), reads=[f"pTx{pts[mb]}"], writes=["pden"])
                        ri = nr % 2
                        nr += 1
                        k.op("dve", lambda e, ri=ri: e.reciprocal(out=rden[ri][:], in_=pden[:, :]), reads=["pden"], writes=[f"rdx{ri}"])
                        for dc in range(2):
                            bo = 4 + (nbo % 2)
                            nbo += 1
                            for mb in range(2):
                                k.op("pe", lambda e, bo=bo, mb=mb, dc=dc, p=pts[mb]: e.matmul(
                                    self.psb[bo][:, :], lhsT=self.v_sb[:, mb, (2 * hd + dc) * 128:(2 * hd + dc + 1) * 128], rhs=pT[p][:],
                                    start=(mb == 0), stop=(mb == 1)), reads=[f"pTx{pts[mb]}"], writes=[f"ps{bo}"])
                            k.op("dve", lambda e, bo=bo, ri=ri, dc=dc: e.tensor_tensor(
                                out=xo[qi][:, dc, tg * 512:(tg + 1) * 512], in0=self.psb[bo][:, :], in1=rden[ri][:], op=ALU.mult),
                                reads=[f"ps{bo}", f"rdx{ri}"], writes=[f"xo{qi}"])

                    xa_s(0)
                    for tg in range(NTG):
                        if tg + 1 < NTG:
                            xa_s(tg + 1)
                        xa_rest(tg)
                    k.dma("sp", self.yt[256 * hd:256 * (hd + 1), :].rearrange("(c p) t -> p c t", p=128), xo[qi][:],
                          reads=[f"xo{qi}"], writes=[("yt", hd)])
                k.phase_end()

    def phase_ffn_up(self, l):
        nc, k = self.nc, self.k
        with ExitStack() as es0:
            xnT = self.xnT
            with ExitStack() as es:
                sb = lambda n, s, d: es.enter_context(self.sbuf(n, s, d))
                wa = [sb(f"wa{i}", [128, 8, 128], BF16) for i in range(2)]
                wg = [sb(f"wg{i}", [128, 8, 128], BF16) for i in range(2)]
                CW = S + 2
                ca = [sb(f"ca{i}", [128, CW], F32) for i in range(2)]
                cg = [sb(f"cg{i}", [128, CW], F32) for i in range(2)]
                ao = [sb(f"ao{i}", [128, S], BF16) for i in range(2)]
                taps = sb("taps", [128, 3, 44], F32)
                w = self.w_up[l]
                for t_ in range(3):
                    k.dma("sp", taps[:, t_, :], self.w_ffconv[l, t_].rearrange("(c p) -> p c", p=128), writes=[("taps", t_)],
                          allow_slow_non_contiguous=True)
                tk = [("taps", t_) for t_ in range(3)]
                nps = 0
                for i in range(22):
                    wi = i % 2
                    self.wload(wa[wi], w, 128 * i, 128, f"wa{wi}")
                    self.wload(wg[wi], w, DFF + 128 * i, 128, f"wg{wi}")
                    bufs = ((ca[wi], f"ca{wi}", wa[wi], f"wa{wi}", i), (cg[wi], f"cg{wi}", wg[wi], f"wg{wi}", 22 + i))
                    for (c_, cn_, w_, wn_, ci) in bufs:
                        k.op("dve", lambda e, c_=c_: e.memset(c_[:, 0:2], 0.0), writes=[(cn_, -1, "A")])
                    for tg in range(NTG):
                        T0 = tg * 512
                        for (c_, cn_, w_, wn_, ci) in bufs:
                            b = nps % 7
                            nps += 1
                            ps = self.psb[b]
                            self.gemm_fm(ps[:, :], w_, 128, xnT, T0, 512, wn_, f"ps{b}")
                            k.op("act", lambda e, c_=c_, ps=ps, ci=ci, T0=T0: e.activation(out=c_[:, T0 + 2:T0 + 514], in_=ps[:, :], func=AF.Copy,
                                                                                          scale=taps[:, 0, ci:ci + 1]),
                                 reads=[f"ps{b}"] + tk, writes=[(cn_, tg, "A")])
                            k.op("dve", lambda e, c_=c_, ps=ps, ci=ci, T0=T0: e.scalar_tensor_tensor(
                                out=c_[:, T0 + 1:T0 + 513], in0=ps[:, :], scalar=taps[:, 1, ci:ci + 1], in1=c_[:, T0 + 1:T0 + 513],
                                op0=ALU.mult, op1=ALU.add),
                                reads=[f"ps{b}", (cn_, tg, "A"), (cn_, tg - 1, "A"), (cn_, tg - 1, "D")] + tk, writes=[(cn_, tg, "D")])
                            k.op("dve", lambda e, c_=c_, ps=ps, ci=ci, T0=T0: e.scalar_tensor_tensor(
                                out=c_[:, T0:T0 + 512], in0=ps[:, :], scalar=taps[:, 2, ci:ci + 1], in1=c_[:, T0:T0 + 512],
                                op0=ALU.mult, op1=ALU.add),
                                reads=[f"ps{b}", (cn_, tg, "D"), (cn_, tg - 1, "D"), (cn_, tg - 1, "A")] + tk, writes=[(cn_, tg, "D")])
                    allk = lambda cn_: [(cn_, tg, "A") for tg in range(-1, NTG)] + [(cn_, tg, "D") for tg in range(NTG)]
                    k.op("act", lambda e, wi=wi: e.activation(out=cg[wi][:, 0:S], in_=cg[wi][:, 0:S], func=AF.Silu),
                         reads=allk(f"cg{wi}"), writes=[(f"cg{wi}", "s")])
                    k.op("pool", lambda e, wi=wi: e.tensor_tensor(out=ao[wi][:], in0=ca[wi][:, 0:S], in1=cg[wi][:, 0:S], op=ALU.mult),
                         reads=[(f"cg{wi}", "s")] + allk(f"ca{wi}") + allk(f"cg{wi}"), writes=[f"ao{wi}"])
                    k.dma("sp", self.actt[128 * i:128 * (i + 1), :], ao[wi][:], reads=[f"ao{wi}"], writes=[("actt", i)])
                k.phase_end()

    def phase_final(self):
        nc, k = self.nc, self.k
        with ExitStack() as es:
            sb = lambda n, s, d: es.enter_context(self.sbuf(n, s, d))
            gb = sb("gbF", [128, D], F32)
            hb = [sb(f"hbF{i}", [128, D], F32) for i in range(2)]
            ob = [sb(f"obF{i}", [128, D], F32) for i in range(2)]
            junk = sb("junkF", [128, D], BF16)
            ss = [sb(f"ssF{i}", [128, 1], F32) for i in range(2)]
            rs = [sb(f"rsF{i}", [128, 1], F32) for i in range(2)]
            k.dma("sp", gb[:], bcast_rows(self.g_final, 128), writes=["gb"])
            hb.append(sb("hbF2", [128, D], F32))
            ob.append(sb("obF2", [128, D], F32))

            def loadf(tb):
                k.dma("sp", hb[tb % 3][:], self.hs[tb * 128:(tb + 1) * 128, :], writes=[f"hb{tb % 3}"])

            loadf(0)
            loadf(1)
            for tb in range(NTB):
                i = tb % 2
                hi = tb % 3
                if tb + 2 < NTB:
                    loadf(tb + 2)
                k.op("act", lambda e, i=i, hi=hi: e.activation(out=junk[:], in_=hb[hi][:], func=AF.Square, accum_out=ss[i][:, 0:1]),
                     reads=[f"hb{hi}"], writes=["junk", f"ss{i}"])
                k.op("act", lambda e, i=i: e.activation(out=ss[i][:], in_=ss[i][:], func=AF.Sqrt, bias=EPS, scale=1.0 / D),
                     reads=[f"ss{i}"], writes=[f"ss{i}"])
                k.op("dve", lambda e, i=i: e.reciprocal(out=rs[i][:], in_=ss[i][:]), reads=[f"ss{i}"], writes=[f"rs{i}"])
                k.op("dve", lambda e, i=i, hi=hi: e.scalar_tensor_tensor(out=ob[hi][:], in0=hb[hi][:], scalar=rs[i][:, 0:1], in1=gb[:],
                                                                        op0=ALU.mult, op1=ALU.mult),
                     reads=[f"hb{hi}", f"rs{i}", "gb"], writes=[f"ob{hi}"])
                k.dma("pool", self.out[tb * 128:(tb + 1) * 128, :], ob[hi][:], reads=[f"ob{hi}"], writes=[("out", tb)])
            k.phase_end()


def host_constants():
    ident = np.eye(128, dtype=np.float32)
    xs = np.arange(MW_W)[None, :] - MW_OFF - np.arange(128)[:, None]
    m = ((xs >= 0) & (xs <= 128)).astype(np.float32)
    m += ((xs >= 0) & (xs % 4 == 0) & (xs <= 512)).astype(np.float32)
    m += ((xs >= 0) & (xs % 16 == 0) & (xs <= 2048)).astype(np.float32)
    invf = (500000.0 ** (-np.arange(0, 16, 2, dtype=np.float32) / 16.0)).astype(np.float32)
    c_invf = np.zeros((16, 2), np.float32)
    c_invf[0:8, 0] = invf
    c_invf[8:16, 0] = invf
    c_invf[0:8, 1] = -1.0
    c_invf[8:16, 1] = 1.0
    rc = np.zeros((128, 2, 16), np.float32)
    t = np.arange(16) + 1
    for c, (wa, wb) in enumerate(((2, 4), (8, 16))):
        rc[0:64, c, :] = 1.0 / np.minimum(t, wa)
        rc[64:128, c, :] = 1.0 / np.minimum(t, wb)
    return {"c_ident": ident, "c_mw": m.astype(np.float32), "c_invf": c_invf, "c_rc": rc}


_CACHE = {}


def kernel(**inputs):
    n = 8
    if "nc" not in _CACHE:
        _CACHE["nc"] = Builder().build()
    nc = _CACHE["nc"]
    consts = host_constants()
    shared = {}
    for name in ("g_mix", "w_in", "b_forget", "w_sconv", "w_pool", "pool_scale", "w_out", "g_xa", "g_mem", "w_xq",
                 "w_xkv", "w_xo", "g_ffn", "w_up", "w_ffconv", "w_down", "g_final"):
        shared[name] = np.ascontiguousarray(np.asarray(inputs[name], dtype=np.float32))
    shared.update(consts)
    x = np.asarray(inputs["x"], dtype=np.float32)
    mem = np.asarray(inputs["mem"], dtype=np.float32)
    pos = np.asarray(inputs["positions"], dtype=np.int32)
    in_maps = []
    for c in range(n):
        m = dict(shared)
        m["x"] = np.ascontiguousarray(x[c])
        m["mem"] = np.ascontiguousarray(mem[c])
        m["positions"] = np.ascontiguousarray(pos[c:c + 1])
        in_maps.append(m)
    res = run_bass_kernel_spmd(nc, in_maps, core_ids=list(range(n)))
    return np.stack([np.asarray(r["out"], dtype=np.float32) for r in res.results], axis=0)
```

```python
import math
from contextlib import ExitStack

import numpy as np
import ml_dtypes
import concourse.bass as bass
import concourse.mybir as mybir
from concourse.bass_utils import run_bass_kernel_spmd

F32 = mybir.dt.float32
BF16 = mybir.dt.bfloat16
I32 = mybir.dt.int32
AF = mybir.ActivationFunctionType
ALU = mybir.AluOpType

ENGS = ("pe", "act", "dve", "pool", "sp")
ENGOBJ = {"pe": "tensor", "act": "scalar", "dve": "vector", "pool": "gpsimd", "sp": "sync"}

S = 4096
D = 1024
NTB = 32
NTG = 8
G = 256
NIN = 2564
DFF = 2816
MEM = 256
EPS = 1e-6
MW_OFF = 384
MW_W = 2944


class Sched:
    NDMA = 24

    def __init__(self, nc):
        self.nc = nc
        self.streams = {e: [] for e in ENGS}
        self.esem = {e: nc.alloc_semaphore(name=f"prog_{e}") for e in ENGS}
        self.ecount = {e: 0 for e in ENGS}
        self.known = {e: {} for e in ENGS}
        self.dsem = [nc.alloc_semaphore(name=f"dma_{i}") for i in range(self.NDMA)]
        self.dcount = [0] * self.NDMA
        self.drr = 0
        self.last_write = {}
        self.reads = {}
        self.nblocks = 0

    def _deps(self, eng, reads, writes, is_dma=False):
        deps = []
        skip = None if is_dma else eng
        for r in reads:
            ev = self.last_write.get(r)
            if ev is not None:
                deps.append(ev)
        for w in writes:
            ev = self.last_write.get(w)
            if ev is not None and ev[2] != skip:
                deps.append(ev)
            for ev in self.reads.get(w, ()):
                if ev[2] != skip:
                    deps.append(ev)
        best = {}
        for (sem, val, e) in deps:
            key = id(sem)
            if key not in best or best[key][1] < val:
                best[key] = (sem, val)
        out = []
        kn = self.known[eng]
        for key, (sem, val) in best.items():
            if kn.get(key, 0) >= val:
                continue
            kn[key] = val
            out.append((sem, val))
        return out

    def _record(self, ev, reads, writes):
        for r in reads:
            self.reads.setdefault(r, []).append(ev)
        for w in writes:
            self.last_write[w] = ev
            self.reads[w] = []

    def op(self, eng, fn, reads=(), writes=()):
        waits = self._deps(eng, reads, writes)
        self.ecount[eng] += 1
        ev = (self.esem[eng], self.ecount[eng], eng)
        self.streams[eng].append((waits, fn, (self.esem[eng], 1)))
        self._record(ev, reads, writes)
        return ev

    def dma(self, q, out, in_, reads=(), writes=(), **kw):
        waits = self._deps(q, reads, writes, is_dma=True)
        i = self.drr
        self.drr = (self.drr + 1) % self.NDMA
        sem = self.dsem[i]
        if self.dcount[i] > 0:
            key = id(sem)
            if self.known[q].get(key, 0) < self.dcount[i]:
                self.known[q][key] = self.dcount[i]
                waits.append((sem, self.dcount[i]))
        self.dcount[i] += 16
        ev = (sem, self.dcount[i], "dma")
        self.streams[q].append(
            (waits, lambda e, out=out, in_=in_, kw=kw: e.dma_start(out=out, in_=in_, **kw), (sem, 16))
        )
        self._record(ev, reads, writes)
        return ev

    def phase_end(self):
        waits = [(self.dsem[i], self.dcount[i]) for i in range(self.NDMA) if self.dcount[i] > 0]
        self.streams["sp"].append((waits, None, None))
        nc = self.nc
        with nc.Block() as block:
            for e in ENGS:
                stream = self.streams[e]

                def body(eng, stream=stream):
                    for waits, fn, inc in stream:
                        for sem, val in waits:
                            eng.wait_ge(sem, val)
                        if fn is not None:
                            fn(eng).then_inc(inc[0], inc[1])

                getattr(block, ENGOBJ[e])(body)
        self.streams = {e: [] for e in ENGS}
        self.last_write = {}
        self.reads = {}
        self.nblocks += 1


def bcast_rows(ap1d, nparts):
    (st, cnt), = ap1d.ap
    return bass.AP(ap1d.tensor, ap1d.offset, [[0, nparts], [st, cnt]])


class Builder:
    def __init__(self, debug=None):
        self.debug = debug or set()
        self.nc = bass.Bass("TRN2", target_bir_lowering=False)
        self.k = None
        self.dbg_outputs = []

    def sbuf(self, name, shape, dt):
        self._uid = getattr(self, "_uid", 0) + 1
        return self.nc.sbuf_tensor(f"{name}_u{self._uid}", shape, dt)

    def dram_in(self, name, shape, dt=F32):
        return self.nc.dram_tensor(name, list(shape), dt, kind="ExternalInput").ap()

    def scratch(self, name, shape, dt):
        if name in self.debug:
            self.dbg_outputs.append(name)
            return self.nc.dram_tensor(name, list(shape), dt, kind="ExternalOutput").ap()
        return self.nc.dram_tensor(name, list(shape), dt).ap()

    def build(self, n_layers=2, stop_after=None):
        nc = self.nc
        self.x = self.dram_in("x", [S, D])
        self.mem = self.dram_in("mem", [MEM, D])
        self.pos = self.dram_in("positions", [1, S], I32)
        L = 2
        self.g_mix = self.dram_in("g_mix", [L, D])
        self.w_in = self.dram_in("w_in", [L, D, NIN])
        self.b_forget = self.dram_in("b_forget", [L, 4])
        self.w_sconv = self.dram_in("w_sconv", [L, 3, G])
        self.w_pool = self.dram_in("w_pool", [L, 4, 64, 64])
        self.pool_scale = self.dram_in("pool_scale", [L, G])
        self.w_out = self.dram_in("w_out", [L, D, D])
        self.g_xa = self.dram_in("g_xa", [L, D])
        self.g_mem = self.dram_in("g_mem", [L, D])
        self.w_xq = self.dram_in("w_xq", [L, D, D])
        self.w_xkv = self.dram_in("w_xkv", [L, D, 2 * D])
        self.w_xo = self.dram_in("w_xo", [L, D, D])
        self.g_ffn = self.dram_in("g_ffn", [L, D])
        self.w_up = self.dram_in("w_up", [L, D, 2 * DFF])
        self.w_ffconv = self.dram_in("w_ffconv", [L, 3, 2 * DFF])
        self.w_down = self.dram_in("w_down", [L, DFF, D])
        self.g_final = self.dram_in("g_final", [D])
        self.c_ident = self.dram_in("c_ident", [128, 128])
        self.c_mw = self.dram_in("c_mw", [128, MW_W])
        self.c_invf = self.dram_in("c_invf", [16, 2])
        self.c_rc = self.dram_in("c_rc", [128, 2, 16])

        self.out = nc.dram_tensor("out", [S, D], F32, kind="ExternalOutput").ap()
        self.hs = self.scratch("hs", [S, D], F32)
        self.qk = self.scratch("qk", [2, 2, 4, 65, S], BF16)
        self.vs = self.scratch("vs", [2, S, G], BF16)
        self.yt = self.scratch("yt", [D, S], BF16)
        self.actt = self.scratch("actt", [DFF, S], BF16)
        self.rope = self.scratch("rope", [2, 16, S], F32)
        self.cn = self.scratch("cn", [128, 128], F32)

        self.k = Sched(nc)
        with ExitStack() as es:
            self.es_global = es
            self.ident_bf = es.enter_context(self.sbuf("ident_bf", [128, 128], BF16))
            self.ident_f = es.enter_context(self.sbuf("ident_f", [128, 128], F32))
            self.ones_bf = es.enter_context(self.sbuf("ones_bf", [128, 128], BF16))
            self.kT_sb = es.enter_context(self.sbuf("kT_sb", [128, 8, MEM], BF16))
            self.v_sb = es.enter_context(self.sbuf("v_sb", [128, 2, D], BF16))
            self.xnT = es.enter_context(self.sbuf("xnT", [128, 8, S], BF16))
            self.psb = [es.enter_context(nc.psum_tensor(f"psb{i}", [128, 512], F32)) for i in range(7)]
            self.pst = es.enter_context(nc.psum_tensor("pst", [128, 1024], BF16))
            self.phase_setup()
            done = False
            for l in range(n_layers):
                steps = [
                    ("mix_proj", lambda l=l: self.phase_mix_proj(l)),
                    ("attn", lambda l=l: self.phase_attn(l)),
                    ("mix_out", lambda l=l: self.phase_outproj(self.yt, 8, self.w_out[l], self.x if l == 0 else self.hs,
                                                               norm_g=self.g_xa[l], pieces_fn=lambda es, l=l: self.xa_kv_pieces(l, es))),
                    ("xa_kv", lambda l=l: self.phase_xa_kv(l)),
                    ("xa", lambda l=l: self.phase_xa(l)),
                    ("xa_out", lambda l=l: self.phase_outproj(self.yt, 8, self.w_xo[l], self.hs, norm_g=self.g_ffn[l])),
                    ("ffn_up", lambda l=l: self.phase_ffn_up(l)),
                    ("ffn_down", lambda l=l: self.phase_outproj(
                        self.actt, 22, self.w_down[l], self.hs,
                        norm_g=(self.g_mix[l + 1] if l + 1 < n_layers else self.g_final), final=(l + 1 == n_layers))),
                ]
                for name, fn in steps:
                    fn()
                    if stop_after == (l, name):
                        done = True
                        break
                if done:
                    break
            if not done and n_layers < 2:
                self.phase_final()
        return nc

    def phase_setup(self):
        nc, k = self.nc, self.k
        with ExitStack() as es:
            sb = lambda n, s, d: es.enter_context(self.sbuf(n, s, d))
            k.dma("sp", self.ident_f[:], self.c_ident, writes=["idf"])
            k.op("dve", lambda e: e.tensor_copy(out=self.ident_bf[:], in_=self.ident_f[:]), reads=["idf"], writes=["idb"])
            k.op("pool", lambda e: e.memset(self.ones_bf[:], 1.0), writes=["ones"])
            posi = sb("posi", [16, S], I32)
            ang = sb("ang", [16, S], F32)
            rr = sb("rr", [16, S], F32)
            ri = sb("ri", [16, S], I32)
            rif = sb("rif", [16, S], F32)
            invf = sb("invf", [16, 2], F32)
            k.dma("sp", posi[:], bass.AP(self.pos.tensor, 0, [[0, 16], [1, S]]), writes=["posi"])
            k.dma("sp", invf[:], self.c_invf, writes=["invf"])
            k.op("dve", lambda e: e.tensor_copy(out=ang[:], in_=posi[:]), reads=["posi"], writes=["ang"])
            k.op("dve", lambda e: e.tensor_scalar(out=ang[:], in0=ang[:], scalar1=invf[:, 0:1], scalar2=None, op0=ALU.mult),
                 reads=["ang", "invf"], writes=["ang"])
            inv2pi = 1.0 / (2.0 * math.pi)
            for t, shift in ((0, 0.25), (1, 0.0)):
                k.op("dve", lambda e, shift=shift: e.tensor_scalar(out=rr[:], in0=ang[:], scalar1=inv2pi, scalar2=shift,
                                                                    op0=ALU.mult, op1=ALU.add), reads=["ang"], writes=["rr"])
                k.op("dve", lambda e: e.tensor_copy(out=ri[:], in_=rr[:]), reads=["rr"], writes=["ri"])
                k.op("pool", lambda e: e.tensor_copy(out=rif[:], in_=ri[:]), reads=["ri"], writes=["rif"])
                k.op("dve", lambda e: e.tensor_tensor(out=rr[:], in0=rr[:], in1=rif[:], op=ALU.subtract),
                     reads=["rr", "rif"], writes=["rr"])
                k.op("act", lambda e: e.activation(out=rr[:], in_=rr[:], func=AF.Sin, scale=2.0 * math.pi), reads=["rr"], writes=["rr"])
                if t == 1:
                    k.op("dve", lambda e: e.tensor_scalar(out=rr[:], in0=rr[:], scalar1=invf[:, 1:2], scalar2=None, op0=ALU.mult),
                         reads=["rr", "invf"], writes=["rr"])
                k.dma("sp", self.rope[t], rr[:], reads=["rr"], writes=[("rope", t)])
            k.phase_end()

    def emit_norm(self, es, src, g_row, xnT, ntb, tag):
        nc, k = self.nc, self.k
        sb = lambda n, s, d: es.enter_context(self.sbuf(n + tag, s, d))
        gb = sb("gb", [128, D], F32)
        hb = [sb(f"hb{i}", [128, D], F32) for i in range(2)]
        junk = sb(tag + "junk", [128, D], BF16)
        ss = [sb(tag + f"ss{i}", [128, 1], F32) for i in range(2)]
        rs = [sb(tag + f"rs{i}", [128, 1], F32) for i in range(2)]
        xn = [sb(tag + f"xn{i}", [128, D], BF16) for i in range(2)]
        k.dma("sp", gb[:], bcast_rows(g_row, 128), writes=[tag + "gb"])
        hb.append(sb("hb2", [128, D], F32))
        nh = len(hb)
        psts = [self.pst[:], self.psb[0][:].bitcast(BF16)]
        pkeys = ["pst", "ps0"]

        def load(tb):
            k.dma("sp", hb[tb % nh][:], src[tb * 128:(tb + 1) * 128, :], writes=[tag + f"hb{tb % nh}"])

        def stage_a(tb):
            i = tb % 2
            hi = tb % nh
            k.op("act", lambda e: e.activation(out=junk[:], in_=hb[hi][:], func=AF.Square, accum_out=ss[i][:, 0:1]),
                 reads=[tag + f"hb{hi}"], writes=[tag + "junk", tag + f"ss{i}"])
            k.op("act", lambda e: e.activation(out=ss[i][:], in_=ss[i][:], func=AF.Sqrt, bias=EPS, scale=1.0 / D),
                 reads=[tag + f"ss{i}"], writes=[tag + f"ss{i}"])
            k.op("dve", lambda e: e.reciprocal(out=rs[i][:], in_=ss[i][:]), reads=[tag + f"ss{i}"], writes=[tag + f"rs{i}"])
            k.op("dve", lambda e: e.scalar_tensor_tensor(out=xn[i][:], in0=hb[hi][:], scalar=rs[i][:, 0:1], in1=gb[:],
                                                         op0=ALU.mult, op1=ALU.mult),
                 reads=[tag + f"hb{hi}", tag + f"rs{i}", tag + "gb"], writes=[tag + f"xn{i}"])

        def stage_b(tb):
            i = tb % 2
            pt = psts[i]
            for kc in range(8):
                k.op("pe", lambda e, kc=kc: e.transpose(pt[:, kc * 128:(kc + 1) * 128], xn[i][:, kc * 128:(kc + 1) * 128],
                                                        self.ident_bf[:]),
                     reads=[tag + f"xn{i}"], writes=[pkeys[i]])
            eng = "act" if tb % 2 == 0 else "dve"
            if eng == "act":
                k.op("act", lambda e: e.copy(out=xnT[:, :, tb * 128:(tb + 1) * 128], in_=pt.rearrange("p (k t) -> p k t", k=8)),
                     reads=[pkeys[i]], writes=[(tag + "xnT", tb)])
            else:
                k.op("dve", lambda e: e.tensor_copy(out=xnT[:, :, tb * 128:(tb + 1) * 128], in_=pt.rearrange("p (k t) -> p k t", k=8)),
                     reads=[pkeys[i]], writes=[(tag + "xnT", tb)])

        load(0)
        if ntb > 1:
            load(1)
        stage_a(0)
        for tb in range(ntb):
            if tb + 2 < ntb:
                load(tb + 2)
            if tb + 1 < ntb:
                stage_a(tb + 1)
            stage_b(tb)

    def wload(self, dst, w2d, c0, n, key, nkc=8):
        src = w2d.rearrange("(kc p) n -> p kc n", p=128)[:, :, c0:c0 + n]
        self.k.dma("pool", dst[:, 0:nkc, 0:n], src, writes=[key])

    def gemm_fm(self, ps_ap, wt, m, xnT, t0, n, wkey, pkey, xkeys=()):
        for kc in range(8):
            self.k.op("pe", lambda e, kc=kc: e.matmul(ps_ap, lhsT=wt[:, kc, 0:m], rhs=xnT[:, kc, t0:t0 + n],
                                                      start=(kc == 0), stop=(kc == 7)),
                      reads=[wkey] + list(xkeys), writes=[pkey])

    def phase_mix_proj(self, l):
        nc, k = self.nc, self.k
        src = self.x if l == 0 else self.hs
        w = self.w_in[l]
        with ExitStack() as es0:
            xnT = self.xnT
            if l == 0:
                with ExitStack() as es:
                    self.emit_norm(es, src, self.g_mix[l], xnT, NTB, "m")
                    k.phase_end()
            with ExitStack() as es:
                sb = lambda n, s, d: es.enter_context(self.sbuf(n, s, d))
                PADW = 16
                stA = sb("stA", [128, PADW + S], F32)
                stB = sb("stB", [128, PADW + S], F32)
                stC = sb("stC", [128, PADW + S], F32)
                yo = [sb(f"yo{i}", [128, S], BF16) for i in range(2)]
                wt = [sb(f"wt{i}", [128, 8, 128], BF16) for i in range(3)]
                wc = sb("wc", [128, 3, 2], F32)
                wp = [sb(f"wp{i}", [128, 128], BF16) for i in range(2)]
                psc = sb("psc", [128, 2], F32)
                rc = sb("rc", [128, 2, 16], F32)
                tmp16 = sb("tmp16", [128, 16], F32)
                for t_, nm in ((stA, "stA"), (stB, "stB"), (stC, "stC")):
                    k.op("pool", lambda e, t_=t_: e.memset(t_[:, 0:PADW], 0.0), writes=[nm])
                for t_ in range(3):
                    k.dma("sp", wc[:, t_, :], self.w_sconv[l, t_].rearrange("(c p) -> p c", p=128), writes=["wc"],
                          allow_slow_non_contiguous=True)
                k.dma("sp", psc[:], self.pool_scale[l].rearrange("(c p) -> p c", p=128), writes=["psc"],
                      allow_slow_non_contiguous=True)
                k.dma("sp", rc[:], self.c_rc, writes=["rc"])
                nps = [0]

                def proj_to(dst_fn, c0, wi):
                    self.wload(wt[wi], w, c0, 128, f"wt{wi}")
                    for tg in range(NTG):
                        b = nps[0] % 6
                        nps[0] += 1
                        self.gemm_fm(self.psb[b][:, :], wt[wi], 128, xnT, tg * 512, 512, f"wt{wi}", f"ps{b}")
                        dst_fn(tg, self.psb[b], f"ps{b}")

                for c in range(2):
                    def ev_c(tg, ps, pk):
                        k.op("act", lambda e: e.copy(out=stA[:, PADW + tg * 512:PADW + (tg + 1) * 512], in_=ps[:, :]),
                             reads=[pk], writes=["stA"])
                    proj_to(ev_c, 512 + 128 * c, 0)

                    def ev_h(tg, ps, pk):
                        sl = slice(PADW + tg * 512, PADW + (tg + 1) * 512)
                        k.op("dve", lambda e: e.tensor_tensor(out=stA[:, sl], in0=ps[:, :], in1=stA[:, sl], op=ALU.mult),
                             reads=[pk, "stA"], writes=["stA"])
                    proj_to(ev_h, 0 + 128 * c, 1)
                    k.op("dve", lambda e, c=c: e.tensor_scalar(out=stB[:, PADW:], in0=stA[:, PADW - 2:PADW - 2 + S], scalar1=wc[:, 0, c:c + 1],
                                                                scalar2=None, op0=ALU.mult), reads=["stA", "wc"], writes=["stB"])
                    k.op("dve", lambda e, c=c: e.scalar_tensor_tensor(out=stB[:, PADW:], in0=stA[:, PADW - 1:PADW - 1 + S],
                                                                       scalar=wc[:, 1, c:c + 1], in1=stB[:, PADW:], op0=ALU.mult, op1=ALU.add),
                         reads=["stA", "wc", "stB"], writes=["stB"])
                    k.op("dve", lambda e, c=c: e.scalar_tensor_tensor(out=stB[:, PADW:], in0=stA[:, PADW:PADW + S],
                                                                       scalar=wc[:, 2, c:c + 1], in1=stB[:, PADW:], op0=ALU.mult, op1=ALU.add),
                         reads=["stA", "wc", "stB"], writes=["stB"])

                    def ev_b(tg, ps, pk, c=c):
                        k.op("dve", lambda e: e.tensor_tensor(out=yo[c][:, tg * 512:(tg + 1) * 512], in0=ps[:, :],
                                                              in1=stB[:, PADW + tg * 512:PADW + (tg + 1) * 512], op=ALU.mult),
                             reads=[pk, "stB"], writes=[f"yo{c}"])
                    proj_to(ev_b, 256 + 128 * c, 2)
                    k.dma("sp", self.yt[128 * c:128 * (c + 1), :], yo[c][:], reads=[f"yo{c}"], writes=[("yt", c)])

                for c in range(2):
                    k.op("pool", lambda e, c=c: e.memset(wp[c][:], 0.0), writes=[f"wp{c}"])
                    k.dma("pool", wp[c][0:64, 0:64], self.w_pool[l, 2 * c], writes=[f"wp{c}"])
                    k.dma("pool", wp[c][64:128, 64:128], self.w_pool[l, 2 * c + 1], writes=[f"wp{c}"])

                    def ev_p(tg, ps, pk):
                        k.op("act", lambda e: e.copy(out=stA[:, PADW + tg * 512:PADW + (tg + 1) * 512], in_=ps[:, :]),
                             reads=[pk], writes=["stA"])
                    proj_to(ev_p, 2308 + 128 * c, c)
                    chain = [(stA, "stA", stB, "stB", 1), (stB, "stB", stC, "stC", 2)]
                    if c == 1:
                        chain += [(stC, "stC", stB, "stB", 4), (stB, "stB", stC, "stC", 8)]
                    for (a, an, b_, bn, sh) in chain:
                        k.op("dve", lambda e, a=a, b_=b_, sh=sh: e.tensor_tensor(out=b_[:, PADW:], in0=a[:, PADW:], in1=a[:, PADW - sh:PADW - sh + S],
                                                                                  op=ALU.add), reads=[an], writes=[bn])
                    halves = [((0, 64), stB, "stB", 2), ((64, 128), stC, "stC", 4)] if c == 0 else \
                             [((0, 64), stB, "stB", 8), ((64, 128), stC, "stC", 16)]
                    z = yo[c]
                    for (p0, p1), sw_, swn, wlen in halves:
                        k.op("dve", lambda e, p0=p0, p1=p1, sw_=sw_, wlen=wlen: e.scalar_tensor_tensor(
                            out=z[p0:p1, :], in0=sw_[p0:p1, PADW:], scalar=1.0 / wlen, in1=stA[p0:p1, PADW:], op0=ALU.mult, op1=ALU.subtract),
                            reads=[swn, "stA"], writes=[f"yo{c}"])
                        k.op("dve", lambda e, p0=p0, p1=p1, sw_=sw_, c=c: e.tensor_tensor(
                            out=tmp16[p0:p1, :], in0=sw_[p0:p1, PADW:PADW + 16], in1=rc[p0:p1, c, :], op=ALU.mult),
                            reads=[swn, "rc"], writes=["tmp16"])
                        k.op("dve", lambda e, p0=p0, p1=p1: e.tensor_tensor(
                            out=z[p0:p1, 0:16], in0=tmp16[p0:p1, :], in1=stA[p0:p1, PADW:PADW + 16], op=ALU.subtract),
                            reads=["tmp16", "stA", f"yo{c}"], writes=[f"yo{c}"])
                    yo2 = stB[:].bitcast(BF16)
                    for tg in range(NTG):
                        b = nps[0] % 6
                        nps[0] += 1
                        k.op("pe", lambda e, b=b, c=c, tg=tg: e.matmul(self.psb[b][:, :], lhsT=wp[c][:, :], rhs=z[:, tg * 512:(tg + 1) * 512],
                                                                       start=True, stop=True), reads=[f"wp{c}", f"yo{c}"], writes=[f"ps{b}"])
                        k.op("act", lambda e, b=b, c=c, tg=tg: e.activation(out=yo2[:, 2 * PADW + tg * 512:2 * PADW + (tg + 1) * 512], in_=self.psb[b][:, :], func=AF.Copy,
                                                                            scale=psc[:, c:c + 1]), reads=[f"ps{b}", "psc", "stB"], writes=["stB"])
                    k.dma("sp", self.yt[768 + 128 * c:768 + 128 * (c + 1), :], yo2[:, 2 * PADW:2 * PADW + S], reads=["stB"], writes=[("yt", 6 + c)])
                k.phase_end()
            with ExitStack() as es:
                sb = lambda n, s, d: es.enter_context(self.sbuf(n, s, d))
                wt = [sb(f"wq{i}", [128, 8, 128], BF16) for i in range(3)]
                wv = sb("wv", [128, 8, 512], BF16)
                NR = 6
                st16 = [sb(f"st16_{i}", [16, 512], F32) for i in range(NR)]
                swp = [sb(f"swp{i}", [16, 512], F32) for i in range(NR)]
                t1 = [sb(f"t1_{i}", [16, 512], F32) for i in range(NR)]
                cs = sb("cs", [16, S], F32)
                sn = sb("sn", [16, S], F32)
                k.dma("sp", cs[:], self.rope[0], writes=["cs"])
                k.dma("sp", sn[:], self.rope[1], writes=["sn"])
                qst = [sb(f"qst{i}", [64, S], BF16) for i in range(4)]
                vst = [sb(f"vst{i}", [128, 512], BF16) for i in range(2)]
                nps = 0
                nrope = 0
                ncs = 0
                nq = 0
                for typ in range(2):
                    for qk_ in range(2):
                        for hp in range(2):
                            c0 = 768 + typ * 768 + qk_ * 256 + 128 * hp
                            wi = nq % 3
                            qa = 2 * (nq % 2)
                            nq += 1
                            self.wload(wt[wi], w, c0, 128, f"wq{wi}")
                            scale = 0.125 if qk_ == 0 else 1.0
                            for tg in range(NTG):
                                b = nps % 6
                                nps += 1
                                self.gemm_fm(self.psb[b][:, :], wt[wi], 128, xnT, tg * 512, 512, f"wq{wi}", f"ps{b}")
                                for hh in range(2):
                                    qi = qa + hh
                                    p0 = 64 * hh
                                    k.op("act", lambda e, b=b, qi=qi, tg=tg, scale=scale, p0=p0: e.activation(
                                        out=qst[qi][:, tg * 512:(tg + 1) * 512], in_=self.psb[b][p0:p0 + 64, :], func=AF.Copy, scale=scale),
                                        reads=[f"ps{b}"], writes=[(f"qst{qi}", tg)])
                                    if typ == 1:
                                        continue
                                    r = nrope % NR
                                    nrope += 1
                                    k.op("act", lambda e, b=b, r=r, scale=scale, p0=p0: e.activation(
                                        out=st16[r][:], in_=self.psb[b][p0:p0 + 16, :], func=AF.Copy, scale=scale),
                                        reads=[f"ps{b}"], writes=[f"st16_{r}"])
                                    k.dma("sp", swp[r][0:8, :], st16[r][8:16, :], reads=[f"st16_{r}"], writes=[f"swp{r}a"])
                                    k.dma("sp", swp[r][8:16, :], st16[r][0:8, :], reads=[f"st16_{r}"], writes=[f"swp{r}b"])
                                    k.op("dve", lambda e, r=r, tg=tg: e.tensor_tensor(out=t1[r][:], in0=swp[r][:], in1=sn[:, tg * 512:(tg + 1) * 512], op=ALU.mult),
                                         reads=[f"swp{r}a", f"swp{r}b", "sn"], writes=[f"t1_{r}"])
                                    k.op("dve", lambda e, r=r, tg=tg: e.tensor_tensor(out=st16[r][:], in0=st16[r][:], in1=cs[:, tg * 512:(tg + 1) * 512], op=ALU.mult),
                                         reads=[f"st16_{r}", "cs", f"swp{r}a", f"swp{r}b"], writes=[f"st16_{r}"])
                                    k.op("dve", lambda e, r=r, qi=qi, tg=tg: e.tensor_tensor(
                                        out=qst[qi][0:16, tg * 512:(tg + 1) * 512], in0=st16[r][:], in1=t1[r][:], op=ALU.add),
                                        reads=[f"st16_{r}", f"t1_{r}", (f"qst{qi}", tg)], writes=[(f"qst{qi}", tg)])
                            for hh in range(2):
                                qi = qa + hh
                                k.dma("sp", self.qk[typ, qk_, 2 * hp + hh, 0:64, :], qst[qi][:], reads=[(f"qst{qi}", tg) for tg in range(NTG)],
                                      writes=[("qk", typ, qk_, hp, hh)])
                src_v = w.rearrange("(kc p) n -> p kc n", p=128)
                k.dma("pool", wv[:, :, 0:256], src_v[:, :, 1280:1536], writes=["wv"])
                k.dma("pool", wv[:, :, 256:512], src_v[:, :, 2048:2304], writes=["wv"])
                for tb in range(NTB):
                    b = nps % 6
                    nps += 1
                    i = tb % 2
                    for kc in range(8):
                        k.op("pe", lambda e, b=b, kc=kc, tb=tb: e.matmul(self.psb[b][:, :], lhsT=xnT[:, kc, tb * 128:(tb + 1) * 128], rhs=wv[:, kc, :],
                                                                         start=(kc == 0), stop=(kc == 7)), reads=["wv"], writes=[f"ps{b}"])
                    k.op("act", lambda e, b=b, i=i: e.copy(out=vst[i][:], in_=self.psb[b][:, :]), reads=[f"ps{b}"], writes=[f"vst{i}"])
                    k.dma("sp", self.vs[0, tb * 128:(tb + 1) * 128, :], vst[i][:, 0:256], reads=[f"vst{i}"], writes=[("vs0", tb)])
                    k.dma("sp", self.vs[1, tb * 128:(tb + 1) * 128, :], vst[i][:, 256:512], reads=[f"vst{i}"], writes=[("vs1", tb)])
                k.phase_end()
            with ExitStack() as es:
                sb = lambda n, s, d: es.enter_context(self.sbuf(n, s, d))
                wf = sb("wf", [128, 8, 4], BF16)
                fl = sb("fl", [4, S], F32)
                cc = sb("cc", [4, S], F32)
                onesf = sb("onesf", [4, S], F32)
                crow = sb("crow", [4, S], BF16)
                negb = sb("negb", [4, 1], F32)
                cnT = sb("cnT", [128, 128], F32)
                nps = 0
                self.wload(wf, w, 2304, 4, "wf")
                k.dma("sp", negb[:], self.b_forget[l].rearrange("(p o) -> p o", o=1), writes=["negb"], allow_slow_non_contiguous=True)
                k.op("dve", lambda e: e.tensor_scalar(out=negb[:], in0=negb[:], scalar1=-1.0, scalar2=None, op0=ALU.mult),
                     reads=["negb"], writes=["negb"])
                k.op("pool", lambda e: e.memset(onesf[:], 1.0), writes=["onesf"])
                for tg in range(NTG):
                    b = nps % 6
                    nps += 1
                    self.gemm_fm(self.psb[b][0:4, :], wf, 4, xnT, tg * 512, 512, "wf", f"ps{b}")
                    k.op("act", lambda e, b=b, tg=tg: e.activation(out=fl[:, tg * 512:(tg + 1) * 512], in_=self.psb[b][0:4, :], func=AF.Exp,
                                                                   scale=-1.0, bias=negb[:, 0:1]), reads=[f"ps{b}", "negb"], writes=["fl"])
                k.op("act", lambda e: e.activation(out=fl[:], in_=fl[:], func=AF.Ln, bias=1.0), reads=["fl"], writes=["fl"])
                k.op("dve", lambda e: e.tensor_scalar(out=fl[:], in0=fl[:], scalar1=-1.0, scalar2=None, op0=ALU.mult), reads=["fl"], writes=["fl"])
                k.op("dve", lambda e: e.tensor_tensor_scan(out=cc[:], data0=onesf[:], data1=fl[:], initial=0.0, op0=ALU.mult, op1=ALU.add),
                     reads=["fl", "onesf"], writes=["cc"])
                k.op("dve", lambda e: e.tensor_copy(out=crow[:], in_=cc[:]), reads=["cc"], writes=["crow"])
                for h in range(4):
                    k.dma("sp", self.qk[1, 0, h, 64:65, :], crow[h:h + 1, :], reads=["crow"], writes=[("qkc", h)])
                k.op("dve", lambda e: e.tensor_copy(out=crow[:], in_=onesf[:]), reads=["onesf", "crow"], writes=["crow"])
                for h in range(4):
                    k.dma("sp", self.qk[1, 1, h, 64:65, :], crow[h:h + 1, :], reads=["crow"], writes=[("qk1", h)])
                pcn = self.psb[6]
                for j in range(NTB):
                    k.op("pe", lambda e, j=j: e.transpose(pcn[:, 4 * j:4 * j + 4], cc[0:4, j * 128:(j + 1) * 128], self.ident_f[0:4, 0:4]),
                         reads=["cc"], writes=["pcn"])
                k.op("dve", lambda e: e.tensor_scalar(out=cnT[:], in0=pcn[:, 0:128], scalar1=-1.0, scalar2=None, op0=ALU.mult),
                     reads=["pcn"], writes=["cnT"])
                k.dma("sp", self.cn, cnT[:], reads=["cnT"], writes=["cn"])
                k.phase_end()

    def phase_attn(self, l):
        nc, k = self.nc, self.k
        with ExitStack() as es:
            sb = lambda n, s, d: es.enter_context(self.sbuf(n, s, d))
            qT = [sb(f"qT{i}", [65, S], BF16) for i in range(2)]
            kT = [sb(f"kT{i}", [65, S], BF16) for i in range(2)]
            vt = [sb(f"vt{i}", [128, NTB, 128], BF16) for i in range(2)]
            oT = [sb(f"oT{i}", [64, S], BF16) for i in range(2)]
            pT = [sb(f"pT{i}", [128, 512], BF16) for i in range(6)]
            rden = [sb(f"rden{i}", [128, 512], F32) for i in range(2)]
            mw = sb("mw", [128, MW_W], BF16)
            cneg = sb("cneg", [128, 128], F32)
            k.dma("pool", mw[:], self.c_mw, writes=["mw"])
            k.dma("sp", cneg[:], self.cn, writes=["cneg"])
            for i in range(2):
                k.op("pool", lambda e, i=i: e.memset(vt[i][:, :, 64:128], 1.0), writes=[f"vt{i}"])
                k.op("pool", lambda e, i=i: e.memset(qT[i][64:65, :], 0.0), writes=[f"qT{i}"])
                k.op("pool", lambda e, i=i: e.memset(kT[i][64:65, :], 0.0), writes=[f"kT{i}"])
            insts = [(t, h) for t in range(2) for h in range(4)]

            def load(n):
                t, h = insts[n]
                i = n % 2
                kr = 65 if t == 1 else 64
                k.dma("sp", qT[i][0:kr, :], self.qk[t, 0, h, 0:kr, :], writes=[f"qT{i}"])
                k.dma("sp", kT[i][0:kr, :], self.qk[t, 1, h, 0:kr, :], writes=[f"kT{i}"])
                k.dma("sp", vt[i][:, :, 0:64], self.vs[t].rearrange("(j p) d -> p j d", p=128)[:, :, 64 * h:64 * (h + 1)], writes=[f"vt{i}"])

            SB = [0, 1, 2, 3, 6]
            LA = 4
            units = []
            nacc = 0
            for n, (t, h) in enumerate(insts):
                for I in range(NTG):
                    jlo = 0 if t == 1 else max(0, 4 * I - 16)
                    a_ = nacc % 2
                    nacc += 1
                    js = list(range(jlo, 4 * I + 4))
                    for idx, j in enumerate(js):
                        c1 = 512
                        if t == 0 and 4 * I - 16 >= 0 and j - (4 * I - 16) < 3 and idx > 0:
                            c1 = 128 * (j - (4 * I - 16) + 1)
                        units.append(dict(n=n, t=t, h=h, I=I, j=j, idx=idx, last=(idx == len(js) - 1), a=a_, c1=c1,
                                          first_of_inst=(I == 0 and idx == 0), last_of_inst=(I == NTG - 1 and idx == len(js) - 1)))

            pending = []

            def emit_s(u, m):
                i = u["n"] % 2
                kr = 65
                I, j = u["I"], u["j"]
                r = j - 4 * I
                c0 = 128 * r if r > 0 else 0
                sbk = SB[m % len(SB)]
                ps_s = self.psb[sbk]
                c1 = u["c1"]
                k.op("pe", lambda e: e.matmul(ps_s[:, c0:c1], lhsT=kT[i][0:kr, j * 128:(j + 1) * 128],
                                              rhs=qT[i][0:kr, I * 512 + c0:I * 512 + c1], start=True, stop=True),
                     reads=[f"kT{i}", f"qT{i}"], writes=[f"ps{sbk}"])

            def emit_rest(u, m):
                i = u["n"] % 2
                t, h, I, j, idx, a_ = u["t"], u["h"], u["I"], u["j"], u["idx"], u["a"]
                r = j - 4 * I
                c0 = 128 * r if r > 0 else 0
                sbk = SB[m % len(SB)]
                ps_s = self.psb[sbk]
                p = m % len(pT)
                c1 = u["c1"]
                oacc = self.psb[4 + a_]
                if u["first_of_inst"] and u["n"] + 1 < len(insts):
                    load(u["n"] + 1)
                if t == 1:
                    k.op("act", lambda e: e.activation(out=pT[p][:, c0:512], in_=ps_s[:, c0:512], func=AF.Exp,
                                                       bias=cneg[:, 4 * j + h:4 * j + h + 1], scale=1.0),
                         reads=[f"ps{sbk}", "cneg"], writes=[f"pT{p}"])
                    if r >= 0:
                        k.op("pool", lambda e: e.affine_select(out=pT[p][:, c0:c0 + 128], in_=pT[p][:, c0:c0 + 128], pattern=[[1, 128]],
                                                               compare_op=ALU.is_ge, fill=0.0, base=0, channel_multiplier=-1),
                             reads=[f"pT{p}"], writes=[f"pT{p}"])
                else:
                    k.op("act", lambda e: e.activation(out=pT[p][:, c0:c1], in_=ps_s[:, c0:c1], func=AF.Exp),
                         reads=[f"ps{sbk}"], writes=[f"pT{p}"])
                    m0 = 512 * I - 128 * j + MW_OFF + c0
                    meng = "pool" if m % 4 == 3 else "dve"
                    k.op(meng, lambda e: e.tensor_tensor(out=pT[p][:, c0:c1], in0=pT[p][:, c0:c1], in1=mw[:, m0:m0 + c1 - c0], op=ALU.mult),
                         reads=[f"pT{p}", "mw"], writes=[f"pT{p}"])
                k.op("pe", lambda e: e.matmul(oacc[:, c0:c1], lhsT=vt[i][:, j, :], rhs=pT[p][:, c0:c1], start=(idx == 0), stop=u["last"]),
                     reads=[f"vt{i}", f"pT{p}"], writes=[f"po{a_}"])
                if u["last"]:
                    if t == 0:
                        k.op("act", lambda e: e.activation(out=rden[a_][64:128, :], in_=oacc[64:128, :], func=AF.Ln),
                             reads=[f"po{a_}"], writes=[f"rden{a_}"])
                        k.op("act", lambda e: e.activation(out=rden[a_][64:128, :], in_=rden[a_][64:128, :], func=AF.Exp, scale=-1.0),
                             reads=[f"rden{a_}"], writes=[f"rden{a_}"])
                    else:
                        k.op("dve", lambda e: e.reciprocal(out=rden[a_][64:128, :], in_=oacc[64:128, :]), reads=[f"po{a_}"], writes=[f"rden{a_}"])
                    k.op("dve", lambda e: e.tensor_tensor(out=oT[i][:, I * 512:(I + 1) * 512], in0=oacc[0:64, :], in1=rden[a_][64:128, :], op=ALU.mult),
                         reads=[f"po{a_}", f"rden{a_}"], writes=[f"oT{i}"])
                if u["last_of_inst"]:
                    row0 = 256 + 256 * t + 64 * h
                    k.dma("sp", self.yt[row0:row0 + 64, :], oT[i][:], reads=[f"oT{i}"], writes=[("yt", "a", t, h)])

            load(0)
            nu = len(units)
            for m in range(nu + LA):
                if m < nu:
                    emit_s(units[m], m)
                if m - LA >= 0:
                    emit_rest(units[m - LA], m - LA)
            k.phase_end()

    def phase_outproj(self, srcT, nkc, w2d, h_src, norm_g=None, final=False, pieces_fn=None):
        nc, k = self.nc, self.k
        xnT = self.xnT
        with ExitStack() as es:
            sb = lambda n, s, d: es.enter_context(self.sbuf(n, s, d))
            wo = sb("wo", [128, nkc, D], BF16)
            yt = [sb(f"ytb{i}", [128, nkc, 512], BF16) for i in range(2)]
            NHB = 4
            hb = [sb(f"hbo{i}", [128, D], F32) for i in range(NHB)]
            gb = sb("gbo", [128, D], F32)
            junk = sb("junko", [128, D], BF16)
            ss = [sb(f"sso{i}", [128, 1], F32) for i in range(2)]
            rs = [sb(f"rso{i}", [128, 1], F32) for i in range(2)]
            if final:
                ob = [sb(f"obo{i}", [128, D], F32) for i in range(2)]
            else:
                xn = [sb(f"xno{i}", [128, D], BF16) for i in range(2)]
            k.dma("sp", gb[:], bcast_rows(norm_g, 128), writes=["gb"])
            wsrc = w2d.rearrange("(kc p) n -> p kc n", p=128)
            step = 4 if nkc <= 8 else 2
            for c in range(0, nkc, step):
                k.dma("pool", wo[:, c:c + step, :], wsrc[:, c:c + step, :], writes=[("wo", c)])
            wkeys = [("wo", c) for c in range(0, nkc, step)]
            ysrc = srcT.rearrange("(kc p) t -> p kc t", p=128)
            k.dma("sp", yt[0][:], ysrc[:, :, 0:512], writes=["ytb0"])

            def loadh(tb):
                k.dma("sp", hb[tb % NHB][:], h_src[tb * 128:(tb + 1) * 128, :], writes=[f"hbo{tb % NHB}"])

            def norm_a1(tb):
                i = tb % 2
                hi = tb % NHB
                k.op("act", lambda e: e.activation(out=junk[:], in_=hb[hi][:], func=AF.Square, accum_out=ss[i][:, 0:1]),
                     reads=[f"hbo{hi}"], writes=["junk", f"ss{i}"])
                k.op("act", lambda e: e.activation(out=ss[i][:], in_=ss[i][:], func=AF.Sqrt, bias=EPS, scale=1.0 / D),
                     reads=[f"ss{i}"], writes=[f"ss{i}"])

            def norm_a2(tb):
                i = tb % 2
                hi = tb % NHB
                k.op("dve", lambda e: e.reciprocal(out=rs[i][:], in_=ss[i][:]), reads=[f"ss{i}"], writes=[f"rs{i}"])
                if final:
                    k.op("dve", lambda e: e.scalar_tensor_tensor(out=ob[i][:], in0=hb[hi][:], scalar=rs[i][:, 0:1], in1=gb[:],
                                                                 op0=ALU.mult, op1=ALU.mult),
                         reads=[f"hbo{hi}", f"rs{i}", "gb"], writes=[f"ob{i}"])
                    k.dma("act", self.out[tb * 128:(tb + 1) * 128, :], ob[i][:], reads=[f"ob{i}"], writes=[("out", tb)])
                else:
                    k.op("dve", lambda e: e.scalar_tensor_tensor(out=xn[i][:], in0=hb[hi][:], scalar=rs[i][:, 0:1], in1=gb[:],
                                                                 op0=ALU.mult, op1=ALU.mult),
                         reads=[f"hbo{hi}", f"rs{i}", "gb"], writes=[f"xn{i}"])

            def norm_b(tb):
                if final:
                    return
                i = tb % 2
                for kc in range(8):
                    k.op("pe", lambda e, kc=kc: e.transpose(self.pst[:, kc * 128:(kc + 1) * 128], xn[i][:, kc * 128:(kc + 1) * 128],
                                                            self.ident_bf[:]),
                         reads=[f"xn{i}"], writes=["pst"])
                k.op("act", lambda e: e.copy(out=xnT[:, :, tb * 128:(tb + 1) * 128], in_=self.pst[:].rearrange("p (k t) -> p k t", k=8)),
                     reads=["pst"], writes=[("xnT", tb)])

            loadh(0)
            loadh(1)
            pieces = pieces_fn(es) if pieces_fn is not None else []
            nb = 0
            for tg in range(NTG):
                yi = tg % 2
                if tg + 1 < NTG:
                    k.dma("sp", yt[1 - yi][:], ysrc[:, :, (tg + 1) * 512:(tg + 2) * 512], writes=[f"ytb{1 - yi}"])
                for tbi in range(4):
                    tb = 4 * tg + tbi
                    hi = tb % NHB
                    if tb + 2 < NTB:
                        loadh(tb + 2)
                    if pieces and tb >= 2 and tb % 2 == 0:
                        pieces.pop(0)()
                    for half in range(2):
                        b = nb % 7
                        nb += 1
                        for kc in range(nkc):
                            k.op("pe", lambda e, b=b, yi=yi, kc=kc, tbi=tbi, half=half: e.matmul(
                                self.psb[b][:, :], lhsT=yt[yi][:, kc, tbi * 128:(tbi + 1) * 128], rhs=wo[:, kc, half * 512:(half + 1) * 512],
                                start=(kc == 0), stop=(kc == nkc - 1)), reads=[f"ytb{yi}"] + (wkeys if kc == 0 else []), writes=[f"ps{b}"])
                        k.op("dve", lambda e, b=b, hi=hi, half=half: e.tensor_tensor(
                            out=hb[hi][:, half * 512:(half + 1) * 512], in0=self.psb[b][:, :], in1=hb[hi][:, half * 512:(half + 1) * 512], op=ALU.add),
                            reads=[f"ps{b}", f"hbo{hi}"], writes=[f"hbo{hi}"])
                    if not final:
                        k.dma("act", self.hs[tb * 128:(tb + 1) * 128, :], hb[hi][:], reads=[f"hbo{hi}"], writes=[("hs", tb)])
                    norm_a1(tb)
                    if tb >= 1:
                        norm_a2(tb - 1)
                    if tb >= 2:
                        norm_b(tb - 2)
            norm_a2(NTB - 1)
            norm_b(NTB - 2)
            norm_b(NTB - 1)
            while pieces:
                pieces.pop(0)()
            k.phase_end()

    def xa_kv_pieces(self, l, es):
        nc, k = self.nc, self.k
        sb = lambda n, s, d: es.enter_context(self.sbuf(n, s, d))
        mT = sb("mT", [128, 8, MEM], BF16)
        wk = [sb(f"wk{i}", [128, 8, 128], BF16) for i in range(2)]
        wv = sb("wvx", [128, 8, D], BF16)
        w = self.w_xkv[l]
        xk = [("kxnT", 0), ("kxnT", 1)]
        pieces = []
        pieces.append(lambda: self.emit_norm(es, self.mem, self.g_mem[l], mT, 2, "k"))

        def kpiece(fc):
            i = fc % 2
            self.wload(wk[i], w, fc * 128, 128, f"wk{i}")
            b = fc % 4
            self.gemm_fm(self.psb[b][:, 0:MEM], wk[i], 128, mT, 0, MEM, f"wk{i}", f"ps{b}", xkeys=xk)
            k.op("act", lambda e: e.copy(out=self.kT_sb[:, fc, :], in_=self.psb[b][:, 0:MEM]), reads=[f"ps{b}"], writes=["kT_sb"])

        for fc in range(8):
            pieces.append(lambda fc=fc: kpiece(fc))

        def vload():
            wsrc = w.rearrange("(kc p) n -> p kc n", p=128)
            k.dma("pool", wv[:, :, 0:512], wsrc[:, :, D:D + 512], writes=["wvx0"])
            k.dma("pool", wv[:, :, 512:1024], wsrc[:, :, D + 512:2 * D], writes=["wvx1"])

        pieces.append(vload)

        def vpiece(mb, half):
            b = 4 + (2 * mb + half) % 2
            for kc in range(8):
                k.op("pe", lambda e, kc=kc: e.matmul(self.psb[b][:, :], lhsT=mT[:, kc, mb * 128:(mb + 1) * 128],
                                                     rhs=wv[:, kc, half * 512:(half + 1) * 512], start=(kc == 0), stop=(kc == 7)),
                     reads=[f"wvx{half}"] + xk, writes=[f"ps{b}"])
            k.op("act", lambda e: e.copy(out=self.v_sb[:, mb, half * 512:(half + 1) * 512], in_=self.psb[b][:, :]),
                 reads=[f"ps{b}"], writes=["v_sb"])

        for mb in range(2):
            for half in range(2):
                pieces.append(lambda mb=mb, half=half: vpiece(mb, half))
        return pieces

    def phase_xa_kv(self, l):
        pass

    def phase_xa(self, l):
        nc, k = self.nc, self.k
        with ExitStack() as es0:
            xnT = self.xnT
            with ExitStack() as es:
                sb = lambda n, s, d: es.enter_context(self.sbuf(n, s, d))
                wq = [sb(f"wxq{i}", [128, 8, 128], BF16) for i in range(2)]
                qTh = [sb(f"qTh{i}", [128, 2, S], BF16) for i in range(2)]
                xo = [sb(f"xo{i}", [128, 2, S], BF16) for i in range(2)]
                pT = [sb(f"pTx{i}", [128, 512], BF16) for i in range(4)]
                rden = [sb(f"rdx{i}", [128, 512], F32) for i in range(2)]
                w = self.w_xq[l]
                nps = 0
                npt = 0
                nr = 0
                nbo = 0
                for hd in range(4):
                    qi = hd % 2
                    for dc in range(2):
                        fc = 2 * hd + dc
                        self.wload(wq[dc], w, fc * 128, 128, f"wxq{dc}")
                        for tg in range(NTG):
                            b = nps % 4
                            nps += 1
                            self.gemm_fm(self.psb[b][:, :], wq[dc], 128, xnT, tg * 512, 512, f"wxq{dc}", f"ps{b}")
                            k.op("act", lambda e, b=b, qi=qi, dc=dc, tg=tg: e.activation(
                                out=qTh[qi][:, dc, tg * 512:(tg + 1) * 512], in_=self.psb[b][:, :], func=AF.Copy, scale=1.0 / 16.0),
                                reads=[f"ps{b}"], writes=[(f"qTh{qi}", dc)])
                    ptss = {}

                    def xa_s(tg, hd=hd, qi=qi):
                        nonlocal nps, npt
                        pts = []
                        for mb in range(2):
                            b = nps % 4
                            nps += 1
                            p = npt % 4
                            npt += 1
                            pts.append(p)
                            for dc in range(2):
                                k.op("pe", lambda e, b=b, dc=dc, mb=mb: e.matmul(
                                    self.psb[b][:, :], lhsT=self.kT_sb[:, 2 * hd + dc, mb * 128:(mb + 1) * 128],
                                    rhs=qTh[qi][:, dc, tg * 512:(tg + 1) * 512], start=(dc == 0), stop=(dc == 1)),
                                    reads=[(f"qTh{qi}", 0), (f"qTh{qi}", 1)], writes=[f"ps{b}"])
                            k.op("act", lambda e, b=b, p=p: e.activation(out=pT[p][:], in_=self.psb[b][:, :], func=AF.Exp),
                                 reads=[f"ps{b}"], writes=[f"pTx{p}"])
                        ptss[tg] = pts

                    def xa_rest(tg, hd=hd, qi=qi):
                        nonlocal nr, nbo
                        pts = ptss[tg]
                        pden = self.psb[6]
                        for mb in range(2):
                            k.op("pe", lambda e, mb=mb, p=pts[mb]: e.matmul(pden[:, :], lhsT=self.ones_bf[:, :], rhs=pT[p][:],
                                                                            start=(mb == 0), stop=(mb == 1)), reads=[f"pTx{pts[mb]}"], writes=["pden"])
                        ri = nr % 2
                        nr += 1
                        k.op("dve", lambda e, ri=ri: e.reciprocal(out=rden[ri][:], in_=pden[:, :]), reads=["pden"], writes=[f"rdx{ri}"])
                        for dc in range(2):
                            bo = 4 + (nbo % 2)
                            nbo += 1
                            for mb in range(2):
                                k.op("pe", lambda e, bo=bo, mb=mb, dc=dc, p=pts[mb]: e.matmul(
                                    self.psb[bo][:, :], lhsT=self.v_sb[:, mb, (2 * hd + dc) * 128:(2 * hd + dc + 1) * 128], rhs=pT[p][:],
                                    start=(mb == 0), stop=(mb == 1)), reads=[f"pTx{pts[mb]}"], writes=[f"ps{bo}"])
                            k.op("dve", lambda e, bo=bo, ri=ri, dc=dc: e.tensor_tensor(
                                out=xo[qi][:, dc, tg * 512:(tg + 1) * 512], in0=self.psb[bo][:, :], in1=rden[ri][:], op=ALU.mult),
                                reads=[f"ps{bo}", f"rdx{ri}"], writes=[f"xo{qi}"])

                    xa_s(0)
                    for tg in range(NTG):
                        if tg + 1 < NTG:
                            xa_s(tg + 1)
                        xa_rest(tg)
                    k.dma("sp", self.yt[256 * hd:256 * (hd + 1), :].rearrange("(c p) t -> p c t", p=128), xo[qi][:],
                          reads=[f"xo{qi}"], writes=[("yt", hd)])
                k.phase_end()

    def phase_ffn_up(self, l):
        nc, k = self.nc, self.k
        with ExitStack() as es0:
            xnT = self.xnT
            with ExitStack() as es:
                sb = lambda n, s, d: es.enter_context(self.sbuf(n, s, d))
                wa = [sb(f"wa{i}", [128, 8, 128], BF16) for i in range(2)]
                wg = [sb(f"wg{i}", [128, 8, 128], BF16) for i in range(2)]
                CW = S + 2
                ca = [sb(f"ca{i}", [128, CW], F32) for i in range(2)]
                cg = [sb(f"cg{i}", [128, CW], F32) for i in range(2)]
                ao = [sb(f"ao{i}", [128, S], BF16) for i in range(2)]
                taps = sb("taps", [128, 3, 44], F32)
                w = self.w_up[l]
                for t_ in range(3):
                    k.dma("sp", taps[:, t_, :], self.w_ffconv[l, t_].rearrange("(c p) -> p c", p=128), writes=[("taps", t_)],
                          allow_slow_non_contiguous=True)
                tk = [("taps", t_) for t_ in range(3)]
                nps = 0
                for i in range(22):
                    wi = i % 2
                    self.wload(wa[wi], w, 128 * i, 128, f"wa{wi}")
                    self.wload(wg[wi], w, DFF + 128 * i, 128, f"wg{wi}")
                    bufs = ((ca[wi], f"ca{wi}", wa[wi], f"wa{wi}", i), (cg[wi], f"cg{wi}", wg[wi], f"wg{wi}", 22 + i))
                    for (c_, cn_, w_, wn_, ci) in bufs:
                        k.op("dve", lambda e, c_=c_: e.memset(c_[:, 0:2], 0.0), writes=[(cn_, -1, "A")])
                    for tg in range(NTG):
                        T0 = tg * 512
                        for (c_, cn_, w_, wn_, ci) in bufs:
                            b = nps % 7
                            nps += 1
                            ps = self.psb[b]
                            self.gemm_fm(ps[:, :], w_, 128, xnT, T0, 512, wn_, f"ps{b}")
                            k.op("act", lambda e, c_=c_, ps=ps, ci=ci, T0=T0: e.activation(out=c_[:, T0 + 2:T0 + 514], in_=ps[:, :], func=AF.Copy,
                                                                                          scale=taps[:, 0, ci:ci + 1]),
                                 reads=[f"ps{b}"] + tk, writes=[(cn_, tg, "A")])
                            k.op("dve", lambda e, c_=c_, ps=ps, ci=ci, T0=T0: e.scalar_tensor_tensor(
                                out=c_[:, T0 + 1:T0 + 513], in0=ps[:, :], scalar=taps[:, 1, ci:ci + 1], in1=c_[:, T0 + 1:T0 + 513],
                                op0=ALU.mult, op1=ALU.add),
                                reads=[f"ps{b}", (cn_, tg, "A"), (cn_, tg - 1, "A"), (cn_, tg - 1, "D")] + tk, writes=[(cn_, tg, "D")])
                            k.op("dve", lambda e, c_=c_, ps=ps, ci=ci, T0=T0: e.scalar_tensor_tensor(
                                out=c_[:, T0:T0 + 512], in0=ps[:, :], scalar=taps[:, 2, ci:ci + 1], in1=c_[:, T0:T0 + 512],
                                op0=ALU.mult, op1=ALU.add),
                                reads=[f"ps{b}", (cn_, tg, "D"), (cn_, tg - 1, "D"), (cn_, tg - 1, "A")] + tk, writes=[(cn_, tg, "D")])
                    allk = lambda cn_: [(cn_, tg, "A") for tg in range(-1, NTG)] + [(cn_, tg, "D") for tg in range(NTG)]
                    k.op("act", lambda e, wi=wi: e.activation(out=cg[wi][:, 0:S], in_=cg[wi][:, 0:S], func=AF.Silu),
                         reads=allk(f"cg{wi}"), writes=[(f"cg{wi}", "s")])
                    k.op("dve", lambda e, wi=wi: e.tensor_tensor(out=ao[wi][:], in0=ca[wi][:, 0:S], in1=cg[wi][:, 0:S], op=ALU.mult),
                         reads=[(f"cg{wi}", "s")] + allk(f"ca{wi}") + allk(f"cg{wi}"), writes=[f"ao{wi}"])
                    k.dma("sp", self.actt[128 * i:128 * (i + 1), :], ao[wi][:], reads=[f"ao{wi}"], writes=[("actt", i)])
                k.phase_end()

    def phase_final(self):
        nc, k = self.nc, self.k
        with ExitStack() as es:
            sb = lambda n, s, d: es.enter_context(self.sbuf(n, s, d))
            gb = sb("gbF", [128, D], F32)
            hb = [sb(f"hbF{i}", [128, D], F32) for i in range(2)]
            ob = [sb(f"obF{i}", [128, D], F32) for i in range(2)]
            junk = sb("junkF", [128, D], BF16)
            ss = [sb(f"ssF{i}", [128, 1], F32) for i in range(2)]
            rs = [sb(f"rsF{i}", [128, 1], F32) for i in range(2)]
            k.dma("sp", gb[:], bcast_rows(self.g_final, 128), writes=["gb"])
            hb.append(sb("hbF2", [128, D], F32))
            ob.append(sb("obF2", [128, D], F32))

            def loadf(tb):
                k.dma("sp", hb[tb % 3][:], self.hs[tb * 128:(tb + 1) * 128, :], writes=[f"hb{tb % 3}"])

            loadf(0)
            loadf(1)
            for tb in range(NTB):
                i = tb % 2
                hi = tb % 3
                if tb + 2 < NTB:
                    loadf(tb + 2)
                k.op("act", lambda e, i=i, hi=hi: e.activation(out=junk[:], in_=hb[hi][:], func=AF.Square, accum_out=ss[i][:, 0:1]),
                     reads=[f"hb{hi}"], writes=["junk", f"ss{i}"])
                k.op("act", lambda e, i=i: e.activation(out=ss[i][:], in_=ss[i][:], func=AF.Sqrt, bias=EPS, scale=1.0 / D),
                     reads=[f"ss{i}"], writes=[f"ss{i}"])
                k.op("dve", lambda e, i=i: e.reciprocal(out=rs[i][:], in_=ss[i][:]), reads=[f"ss{i}"], writes=[f"rs{i}"])
                k.op("dve", lambda e, i=i, hi=hi: e.scalar_tensor_tensor(out=ob[hi][:], in0=hb[hi][:], scalar=rs[i][:, 0:1], in1=gb[:],
                                                                        op0=ALU.mult, op1=ALU.mult),
                     reads=[f"hb{hi}", f"rs{i}", "gb"], writes=[f"ob{hi}"])
                k.dma("pool", self.out[tb * 128:(tb + 1) * 128, :], ob[hi][:], reads=[f"ob{hi}"], writes=[("out", tb)])
            k.phase_end()


def host_constants():
    ident = np.eye(128, dtype=np.float32)
    xs = np.arange(MW_W)[None, :] - MW_OFF - np.arange(128)[:, None]
    m = ((xs >= 0) & (xs <= 128)).astype(np.float32)
    m += ((xs >= 0) & (xs % 4 == 0) & (xs <= 512)).astype(np.float32)
    m += ((xs >= 0) & (xs % 16 == 0) & (xs <= 2048)).astype(np.float32)
    invf = (500000.0 ** (-np.arange(0, 16, 2, dtype=np.float32) / 16.0)).astype(np.float32)
    c_invf = np.zeros((16, 2), np.float32)
    c_invf[0:8, 0] = invf
    c_invf[8:16, 0] = invf
    c_invf[0:8, 1] = -1.0
    c_invf[8:16, 1] = 1.0
    rc = np.zeros((128, 2, 16), np.float32)
    t = np.arange(16) + 1
    for c, (wa, wb) in enumerate(((2, 4), (8, 16))):
        rc[0:64, c, :] = 1.0 / np.minimum(t, wa)
        rc[64:128, c, :] = 1.0 / np.minimum(t, wb)
    return {"c_ident": ident, "c_mw": m.astype(np.float32), "c_invf": c_invf, "c_rc": rc}


_CACHE = {}


def kernel(**inputs):
    n = 8
    if "nc" not in _CACHE:
        _CACHE["nc"] = Builder().build()
    nc = _CACHE["nc"]
    consts = host_constants()
    shared = {}
    for name in ("g_mix", "w_in", "b_forget", "w_sconv", "w_pool", "pool_scale", "w_out", "g_xa", "g_mem", "w_xq",
                 "w_xkv", "w_xo", "g_ffn", "w_up", "w_ffconv", "w_down", "g_final"):
        shared[name] = np.ascontiguousarray(np.asarray(inputs[name], dtype=np.float32))
    shared.update(consts)
    x = np.asarray(inputs["x"], dtype=np.float32)
    mem = np.asarray(inputs["mem"], dtype=np.float32)
    pos = np.asarray(inputs["positions"], dtype=np.int32)
    in_maps = []
    for c in range(n):
        m = dict(shared)
        m["x"] = np.ascontiguousarray(x[c])
        m["mem"] = np.ascontiguousarray(mem[c])
        m["positions"] = np.ascontiguousarray(pos[c:c + 1])
        in_maps.append(m)
    res = run_bass_kernel_spmd(nc, in_maps, core_ids=list(range(n)))
    return np.stack([np.asarray(r["out"], dtype=np.float32) for r in res.results], axis=0)
```

```python
import math
from contextlib import ExitStack

import numpy as np
import ml_dtypes
import concourse.bass as bass
import concourse.mybir as mybir
from concourse.bass_utils import run_bass_kernel_spmd

F32 = mybir.dt.float32
BF16 = mybir.dt.bfloat16
I32 = mybir.dt.int32
AF = mybir.ActivationFunctionType
ALU = mybir.AluOpType

ENGS = ("pe", "act", "dve", "pool", "sp")
ENGOBJ = {"pe": "tensor", "act": "scalar", "dve": "vector", "pool": "gpsimd", "sp": "sync"}

S = 4096
D = 1024
NTB = 32
NTG = 8
G = 256
NIN = 2564
DFF = 2816
MEM = 256
EPS = 1e-6
MW_OFF = 384
MW_W = 2944


class Sched:
    NDMA = 24

    def __init__(self, nc):
        self.nc = nc
        self.streams = {e: [] for e in ENGS}
        self.esem = {e: nc.alloc_semaphore(name=f"prog_{e}") for e in ENGS}
        self.ecount = {e: 0 for e in ENGS}
        self.known = {e: {} for e in ENGS}
        self.dsem = [nc.alloc_semaphore(name=f"dma_{i}") for i in range(self.NDMA)]
        self.dcount = [0] * self.NDMA
        self.drr = 0
        self.last_write = {}
        self.reads = {}
        self.nblocks = 0

    def _deps(self, eng, reads, writes, is_dma=False):
        deps = []
        skip = None if is_dma else eng
        for r in reads:
            ev = self.last_write.get(r)
            if ev is not None:
                deps.append(ev)
        for w in writes:
            ev = self.last_write.get(w)
            if ev is not None and ev[2] != skip:
                deps.append(ev)
            for ev in self.reads.get(w, ()):
                if ev[2] != skip:
                    deps.append(ev)
        best = {}
        for (sem, val, e) in deps:
            key = id(sem)
            if key not in best or best[key][1] < val:
                best[key] = (sem, val)
        out = []
        kn = self.known[eng]
        for key, (sem, val) in best.items():
            if kn.get(key, 0) >= val:
                continue
            kn[key] = val
            out.append((sem, val))
        return out

    def _record(self, ev, reads, writes):
        for r in reads:
            self.reads.setdefault(r, []).append(ev)
        for w in writes:
            self.last_write[w] = ev
            self.reads[w] = []

    def op(self, eng, fn, reads=(), writes=()):
        waits = self._deps(eng, reads, writes)
        self.ecount[eng] += 1
        ev = (self.esem[eng], self.ecount[eng], eng)
        self.streams[eng].append((waits, fn, (self.esem[eng], 1)))
        self._record(ev, reads, writes)
        return ev

    def dma(self, q, out, in_, reads=(), writes=(), **kw):
        waits = self._deps(q, reads, writes, is_dma=True)
        i = self.drr
        self.drr = (self.drr + 1) % self.NDMA
        sem = self.dsem[i]
        if self.dcount[i] > 0:
            key = id(sem)
            if self.known[q].get(key, 0) < self.dcount[i]:
                self.known[q][key] = self.dcount[i]
                waits.append((sem, self.dcount[i]))
        self.dcount[i] += 16
        ev = (sem, self.dcount[i], "dma")
        self.streams[q].append(
            (waits, lambda e, out=out, in_=in_, kw=kw: e.dma_start(out=out, in_=in_, **kw), (sem, 16))
        )
        self._record(ev, reads, writes)
        return ev

    def phase_end(self):
        waits = [(self.dsem[i], self.dcount[i]) for i in range(self.NDMA) if self.dcount[i] > 0]
        self.streams["sp"].append((waits, None, None))
        nc = self.nc
        with nc.Block() as block:
            for e in ENGS:
                stream = self.streams[e]

                def body(eng, stream=stream):
                    for waits, fn, inc in stream:
                        for sem, val in waits:
                            eng.wait_ge(sem, val)
                        if fn is not None:
                            fn(eng).then_inc(inc[0], inc[1])

                getattr(block, ENGOBJ[e])(body)
        self.streams = {e: [] for e in ENGS}
        self.last_write = {}
        self.reads = {}
        self.nblocks += 1


def bcast_rows(ap1d, nparts):
    (st, cnt), = ap1d.ap
    return bass.AP(ap1d.tensor, ap1d.offset, [[0, nparts], [st, cnt]])


class Builder:
    def __init__(self, debug=None):
        self.debug = debug or set()
        self.nc = bass.Bass("TRN2", target_bir_lowering=False)
        self.k = None
        self.dbg_outputs = []

    def sbuf(self, name, shape, dt):
        self._uid = getattr(self, "_uid", 0) + 1
        return self.nc.sbuf_tensor(f"{name}_u{self._uid}", shape, dt)

    def dram_in(self, name, shape, dt=F32):
        return self.nc.dram_tensor(name, list(shape), dt, kind="ExternalInput").ap()

    def scratch(self, name, shape, dt):
        if name in self.debug:
            self.dbg_outputs.append(name)
            return self.nc.dram_tensor(name, list(shape), dt, kind="ExternalOutput").ap()
        return self.nc.dram_tensor(name, list(shape), dt).ap()

    def build(self, n_layers=2, stop_after=None):
        nc = self.nc
        self.x = self.dram_in("x", [S, D])
        self.mem = self.dram_in("mem", [MEM, D])
        self.pos = self.dram_in("positions", [1, S], I32)
        L = 2
        self.g_mix = self.dram_in("g_mix", [L, D])
        self.w_in = self.dram_in("w_in", [L, D, NIN])
        self.b_forget = self.dram_in("b_forget", [L, 4])
        self.w_sconv = self.dram_in("w_sconv", [L, 3, G])
        self.w_pool = self.dram_in("w_pool", [L, 4, 64, 64])
        self.pool_scale = self.dram_in("pool_scale", [L, G])
        self.w_out = self.dram_in("w_out", [L, D, D])
        self.g_xa = self.dram_in("g_xa", [L, D])
        self.g_mem = self.dram_in("g_mem", [L, D])
        self.w_xq = self.dram_in("w_xq", [L, D, D])
        self.w_xkv = self.dram_in("w_xkv", [L, D, 2 * D])
        self.w_xo = self.dram_in("w_xo", [L, D, D])
        self.g_ffn = self.dram_in("g_ffn", [L, D])
        self.w_up = self.dram_in("w_up", [L, D, 2 * DFF])
        self.w_ffconv = self.dram_in("w_ffconv", [L, 3, 2 * DFF])
        self.w_down = self.dram_in("w_down", [L, DFF, D])
        self.g_final = self.dram_in("g_final", [D])
        self.c_ident = self.dram_in("c_ident", [128, 128])
        self.c_mw = self.dram_in("c_mw", [128, MW_W])
        self.c_invf = self.dram_in("c_invf", [16, 2])
        self.c_rc = self.dram_in("c_rc", [128, 2, 16])

        self.out = nc.dram_tensor("out", [S, D], F32, kind="ExternalOutput").ap()
        self.hs = self.scratch("hs", [S, D], F32)
        self.qk = self.scratch("qk", [2, 2, 4, 65, S], BF16)
        self.vs = self.scratch("vs", [2, S, G], BF16)
        self.yt = self.scratch("yt", [D, S], BF16)
        self.actt = self.scratch("actt", [DFF, S], BF16)
        self.rope = self.scratch("rope", [2, 16, S], F32)
        self.cn = self.scratch("cn", [128, 128], F32)

        self.k = Sched(nc)
        with ExitStack() as es:
            self.es_global = es
            self.ident_bf = es.enter_context(self.sbuf("ident_bf", [128, 128], BF16))
            self.ident_f = es.enter_context(self.sbuf("ident_f", [128, 128], F32))
            self.ones_bf = es.enter_context(self.sbuf("ones_bf", [128, 128], BF16))
            self.kT_sb = es.enter_context(self.sbuf("kT_sb", [128, 8, MEM], BF16))
            self.v_sb = es.enter_context(self.sbuf("v_sb", [128, 2, D], BF16))
            self.xnT = es.enter_context(self.sbuf("xnT", [128, 8, S], BF16))
            self.psb = [es.enter_context(nc.psum_tensor(f"psb{i}", [128, 512], F32)) for i in range(7)]
            self.pst = es.enter_context(nc.psum_tensor("pst", [128, 1024], BF16))
            self.phase_setup()
            done = False
            for l in range(n_layers):
                steps = [
                    ("mix_proj", lambda l=l: self.phase_mix_proj(l)),
                    ("attn", lambda l=l: self.phase_attn(l)),
                    ("mix_out", lambda l=l: self.phase_outproj(self.yt, 8, self.w_out[l], self.x if l == 0 else self.hs,
                                                               norm_g=self.g_xa[l], pieces_fn=lambda es, l=l: self.xa_kv_pieces(l, es))),
                    ("xa_kv", lambda l=l: self.phase_xa_kv(l)),
                    ("xa", lambda l=l: self.phase_xa(l)),
                    ("xa_out", lambda l=l: self.phase_outproj(self.yt, 8, self.w_xo[l], self.hs, norm_g=self.g_ffn[l])),
                    ("ffn_up", lambda l=l: self.phase_ffn_up(l)),
                    ("ffn_down", lambda l=l: self.phase_outproj(
                        self.actt, 22, self.w_down[l], self.hs,
                        norm_g=(self.g_mix[l + 1] if l + 1 < n_layers else self.g_final), final=(l + 1 == n_layers))),
                ]
                for name, fn in steps:
                    fn()
                    if stop_after == (l, name):
                        done = True
                        break
                if done:
                    break
            if not done and n_layers < 2:
                self.phase_final()
        return nc

    def phase_setup(self):
        nc, k = self.nc, self.k
        with ExitStack() as es:
            sb = lambda n, s, d: es.enter_context(self.sbuf(n, s, d))
            k.dma("sp", self.ident_f[:], self.c_ident, writes=["idf"])
            k.op("dve", lambda e: e.tensor_copy(out=self.ident_bf[:], in_=self.ident_f[:]), reads=["idf"], writes=["idb"])
            k.op("pool", lambda e: e.memset(self.ones_bf[:], 1.0), writes=["ones"])
            posi = sb("posi", [16, S], I32)
            ang = sb("ang", [16, S], F32)
            rr = sb("rr", [16, S], F32)
            ri = sb("ri", [16, S], I32)
            rif = sb("rif", [16, S], F32)
            invf = sb("invf", [16, 2], F32)
            k.dma("sp", posi[:], bass.AP(self.pos.tensor, 0, [[0, 16], [1, S]]), writes=["posi"])
            k.dma("sp", invf[:], self.c_invf, writes=["invf"])
            k.op("dve", lambda e: e.tensor_copy(out=ang[:], in_=posi[:]), reads=["posi"], writes=["ang"])
            k.op("dve", lambda e: e.tensor_scalar(out=ang[:], in0=ang[:], scalar1=invf[:, 0:1], scalar2=None, op0=ALU.mult),
                 reads=["ang", "invf"], writes=["ang"])
            inv2pi = 1.0 / (2.0 * math.pi)
            for t, shift in ((0, 0.25), (1, 0.0)):
                k.op("dve", lambda e, shift=shift: e.tensor_scalar(out=rr[:], in0=ang[:], scalar1=inv2pi, scalar2=shift,
                                                                    op0=ALU.mult, op1=ALU.add), reads=["ang"], writes=["rr"])
                k.op("dve", lambda e: e.tensor_copy(out=ri[:], in_=rr[:]), reads=["rr"], writes=["ri"])
                k.op("dve", lambda e: e.tensor_copy(out=rif[:], in_=ri[:]), reads=["ri"], writes=["rif"])
                k.op("dve", lambda e: e.tensor_tensor(out=rr[:], in0=rr[:], in1=rif[:], op=ALU.subtract),
                     reads=["rr", "rif"], writes=["rr"])
                k.op("act", lambda e: e.activation(out=rr[:], in_=rr[:], func=AF.Sin, scale=2.0 * math.pi), reads=["rr"], writes=["rr"])
                if t == 1:
                    k.op("dve", lambda e: e.tensor_scalar(out=rr[:], in0=rr[:], scalar1=invf[:, 1:2], scalar2=None, op0=ALU.mult),
                         reads=["rr", "invf"], writes=["rr"])
                k.dma("sp", self.rope[t], rr[:], reads=["rr"], writes=[("rope", t)])
            k.phase_end()

    def emit_norm(self, es, src, g_row, xnT, ntb, tag):
        nc, k = self.nc, self.k
        sb = lambda n, s, d: es.enter_context(self.sbuf(n + tag, s, d))
        gb = sb("gb", [128, D], F32)
        hb = [sb(f"hb{i}", [128, D], F32) for i in range(2)]
        junk = sb(tag + "junk", [128, D], BF16)
        ss = [sb(tag + f"ss{i}", [128, 1], F32) for i in range(2)]
        rs = [sb(tag + f"rs{i}", [128, 1], F32) for i in range(2)]
        xn = [sb(tag + f"xn{i}", [128, D], BF16) for i in range(2)]
        k.dma("sp", gb[:], bcast_rows(g_row, 128), writes=[tag + "gb"])
        hb.append(sb("hb2", [128, D], F32))
        nh = len(hb)
        psts = [self.pst[:], self.psb[0][:].bitcast(BF16)]
        pkeys = ["pst", "ps0"]

        def load(tb):
            k.dma("sp", hb[tb % nh][:], src[tb * 128:(tb + 1) * 128, :], writes=[tag + f"hb{tb % nh}"])

        def stage_a(tb):
            i = tb % 2
            hi = tb % nh
            k.op("act", lambda e: e.activation(out=junk[:], in_=hb[hi][:], func=AF.Square, accum_out=ss[i][:, 0:1]),
                 reads=[tag + f"hb{hi}"], writes=[tag + "junk", tag + f"ss{i}"])
            k.op("act", lambda e: e.activation(out=ss[i][:], in_=ss[i][:], func=AF.Sqrt, bias=EPS, scale=1.0 / D),
                 reads=[tag + f"ss{i}"], writes=[tag + f"ss{i}"])
            k.op("dve", lambda e: e.reciprocal(out=rs[i][:], in_=ss[i][:]), reads=[tag + f"ss{i}"], writes=[tag + f"rs{i}"])
            k.op("dve", lambda e: e.scalar_tensor_tensor(out=xn[i][:], in0=hb[hi][:], scalar=rs[i][:, 0:1], in1=gb[:],
                                                         op0=ALU.mult, op1=ALU.mult),
                 reads=[tag + f"hb{hi}", tag + f"rs{i}", tag + "gb"], writes=[tag + f"xn{i}"])

        def stage_b(tb):
            i = tb % 2
            pt = psts[i]
            for kc in range(8):
                k.op("pe", lambda e, kc=kc: e.transpose(pt[:, kc * 128:(kc + 1) * 128], xn[i][:, kc * 128:(kc + 1) * 128],
                                                        self.ident_bf[:]),
                     reads=[tag + f"xn{i}"], writes=[pkeys[i]])
            eng = "act" if tb % 2 == 0 else "dve"
            if eng == "act":
                k.op("act", lambda e: e.copy(out=xnT[:, :, tb * 128:(tb + 1) * 128], in_=pt.rearrange("p (k t) -> p k t", k=8)),
                     reads=[pkeys[i]], writes=[(tag + "xnT", tb)])
            else:
                k.op("dve", lambda e: e.tensor_copy(out=xnT[:, :, tb * 128:(tb + 1) * 128], in_=pt.rearrange("p (k t) -> p k t", k=8)),
                     reads=[pkeys[i]], writes=[(tag + "xnT", tb)])

        load(0)
        if ntb > 1:
            load(1)
        stage_a(0)
        for tb in range(ntb):
            if tb + 2 < ntb:
                load(tb + 2)
            if tb + 1 < ntb:
                stage_a(tb + 1)
            stage_b(tb)

    def wload(self, dst, w2d, c0, n, key, nkc=8):
        src = w2d.rearrange("(kc p) n -> p kc n", p=128)[:, :, c0:c0 + n]
        self.k.dma("pool", dst[:, 0:nkc, 0:n], src, writes=[key])

    def gemm_fm(self, ps_ap, wt, m, xnT, t0, n, wkey, pkey, xkeys=()):
        for kc in range(8):
            self.k.op("pe", lambda e, kc=kc: e.matmul(ps_ap, lhsT=wt[:, kc, 0:m], rhs=xnT[:, kc, t0:t0 + n],
                                                      start=(kc == 0), stop=(kc == 7)),
                      reads=[wkey] + list(xkeys), writes=[pkey])

    def phase_mix_proj(self, l):
        nc, k = self.nc, self.k
        src = self.x if l == 0 else self.hs
        w = self.w_in[l]
        with ExitStack() as es0:
            xnT = self.xnT
            if l == 0:
                with ExitStack() as es:
                    self.emit_norm(es, src, self.g_mix[l], xnT, NTB, "m")
                    k.phase_end()
            with ExitStack() as es:
                sb = lambda n, s, d: es.enter_context(self.sbuf(n, s, d))
                PADW = 16
                stA = sb("stA", [128, PADW + S], F32)
                stB = sb("stB", [128, PADW + S], F32)
                stC = sb("stC", [128, PADW + S], F32)
                yo = [sb(f"yo{i}", [128, S], BF16) for i in range(2)]
                wt = [sb(f"wt{i}", [128, 8, 128], BF16) for i in range(3)]
                wc = sb("wc", [128, 3, 2], F32)
                wp = [sb(f"wp{i}", [128, 128], BF16) for i in range(2)]
                psc = sb("psc", [128, 2], F32)
                rc = sb("rc", [128, 2, 16], F32)
                tmp16 = sb("tmp16", [128, 16], F32)
                for t_, nm in ((stA, "stA"), (stB, "stB"), (stC, "stC")):
                    k.op("pool", lambda e, t_=t_: e.memset(t_[:, 0:PADW], 0.0), writes=[nm])
                for t_ in range(3):
                    k.dma("sp", wc[:, t_, :], self.w_sconv[l, t_].rearrange("(c p) -> p c", p=128), writes=["wc"],
                          allow_slow_non_contiguous=True)
                k.dma("sp", psc[:], self.pool_scale[l].rearrange("(c p) -> p c", p=128), writes=["psc"],
                      allow_slow_non_contiguous=True)
                k.dma("sp", rc[:], self.c_rc, writes=["rc"])
                nps = [0]

                def proj_to(dst_fn, c0, wi):
                    self.wload(wt[wi], w, c0, 128, f"wt{wi}")
                    for tg in range(NTG):
                        b = nps[0] % 6
                        nps[0] += 1
                        self.gemm_fm(self.psb[b][:, :], wt[wi], 128, xnT, tg * 512, 512, f"wt{wi}", f"ps{b}")
                        dst_fn(tg, self.psb[b], f"ps{b}")

                for c in range(2):
                    def ev_c(tg, ps, pk):
                        k.op("act", lambda e: e.copy(out=stA[:, PADW + tg * 512:PADW + (tg + 1) * 512], in_=ps[:, :]),
                             reads=[pk], writes=["stA"])
                    proj_to(ev_c, 512 + 128 * c, 0)

                    def ev_h(tg, ps, pk):
                        sl = slice(PADW + tg * 512, PADW + (tg + 1) * 512)
                        k.op("dve", lambda e: e.tensor_tensor(out=stA[:, sl], in0=ps[:, :], in1=stA[:, sl], op=ALU.mult),
                             reads=[pk, "stA"], writes=["stA"])
                    proj_to(ev_h, 0 + 128 * c, 1)
                    k.op("dve", lambda e, c=c: e.tensor_scalar(out=stB[:, PADW:], in0=stA[:, PADW - 2:PADW - 2 + S], scalar1=wc[:, 0, c:c + 1],
                                                                scalar2=None, op0=ALU.mult), reads=["stA", "wc"], writes=["stB"])
                    k.op("dve", lambda e, c=c: e.scalar_tensor_tensor(out=stB[:, PADW:], in0=stA[:, PADW - 1:PADW - 1 + S],
                                                                       scalar=wc[:, 1, c:c + 1], in1=stB[:, PADW:], op0=ALU.mult, op1=ALU.add),
                         reads=["stA", "wc", "stB"], writes=["stB"])
                    k.op("dve", lambda e, c=c: e.scalar_tensor_tensor(out=stB[:, PADW:], in0=stA[:, PADW:PADW + S],
                                                                       scalar=wc[:, 2, c:c + 1], in1=stB[:, PADW:], op0=ALU.mult, op1=ALU.add),
                         reads=["stA", "wc", "stB"], writes=["stB"])

                    def ev_b(tg, ps, pk, c=c):
                        k.op("dve", lambda e: e.tensor_tensor(out=yo[c][:, tg * 512:(tg + 1) * 512], in0=ps[:, :],
                                                              in1=stB[:, PADW + tg * 512:PADW + (tg + 1) * 512], op=ALU.mult),
                             reads=[pk, "stB"], writes=[f"yo{c}"])
                    proj_to(ev_b, 256 + 128 * c, 2)
                    k.dma("sp", self.yt[128 * c:128 * (c + 1), :], yo[c][:], reads=[f"yo{c}"], writes=[("yt", c)])

                for c in range(2):
                    k.op("pool", lambda e, c=c: e.memset(wp[c][:], 0.0), writes=[f"wp{c}"])
                    k.dma("pool", wp[c][0:64, 0:64], self.w_pool[l, 2 * c], writes=[f"wp{c}"])
                    k.dma("pool", wp[c][64:128, 64:128], self.w_pool[l, 2 * c + 1], writes=[f"wp{c}"])

                    def ev_p(tg, ps, pk):
                        k.op("act", lambda e: e.copy(out=stA[:, PADW + tg * 512:PADW + (tg + 1) * 512], in_=ps[:, :]),
                             reads=[pk], writes=["stA"])
                    proj_to(ev_p, 2308 + 128 * c, c)
                    chain = [(stA, "stA", stB, "stB", 1), (stB, "stB", stC, "stC", 2)]
                    if c == 1:
                        chain += [(stC, "stC", stB, "stB", 4), (stB, "stB", stC, "stC", 8)]
                    for (a, an, b_, bn, sh) in chain:
                        k.op("dve", lambda e, a=a, b_=b_, sh=sh: e.tensor_tensor(out=b_[:, PADW:], in0=a[:, PADW:], in1=a[:, PADW - sh:PADW - sh + S],
                                                                                  op=ALU.add), reads=[an], writes=[bn])
                    halves = [((0, 64), stB, "stB", 2), ((64, 128), stC, "stC", 4)] if c == 0 else \
                             [((0, 64), stB, "stB", 8), ((64, 128), stC, "stC", 16)]
                    z = yo[c]
                    for (p0, p1), sw_, swn, wlen in halves:
                        k.op("dve", lambda e, p0=p0, p1=p1, sw_=sw_, wlen=wlen: e.scalar_tensor_tensor(
                            out=z[p0:p1, :], in0=sw_[p0:p1, PADW:], scalar=1.0 / wlen, in1=stA[p0:p1, PADW:], op0=ALU.mult, op1=ALU.subtract),
                            reads=[swn, "stA"], writes=[f"yo{c}"])
                        k.op("dve", lambda e, p0=p0, p1=p1, sw_=sw_, c=c: e.tensor_tensor(
                            out=tmp16[p0:p1, :], in0=sw_[p0:p1, PADW:PADW + 16], in1=rc[p0:p1, c, :], op=ALU.mult),
                            reads=[swn, "rc"], writes=["tmp16"])
                        k.op("dve", lambda e, p0=p0, p1=p1: e.tensor_tensor(
                            out=z[p0:p1, 0:16], in0=tmp16[p0:p1, :], in1=stA[p0:p1, PADW:PADW + 16], op=ALU.subtract),
                            reads=["tmp16", "stA", f"yo{c}"], writes=[f"yo{c}"])
                    yo2 = stB[:].bitcast(BF16)
                    for tg in range(NTG):
                        b = nps[0] % 6
                        nps[0] += 1
                        k.op("pe", lambda e, b=b, c=c, tg=tg: e.matmul(self.psb[b][:, :], lhsT=wp[c][:, :], rhs=z[:, tg * 512:(tg + 1) * 512],
                                                                       start=True, stop=True), reads=[f"wp{c}", f"yo{c}"], writes=[f"ps{b}"])
                        k.op("act", lambda e, b=b, c=c, tg=tg: e.activation(out=yo2[:, 2 * PADW + tg * 512:2 * PADW + (tg + 1) * 512], in_=self.psb[b][:, :], func=AF.Copy,
                                                                            scale=psc[:, c:c + 1]), reads=[f"ps{b}", "psc", "stB"], writes=["stB"])
                    k.dma("sp", self.yt[768 + 128 * c:768 + 128 * (c + 1), :], yo2[:, 2 * PADW:2 * PADW + S], reads=["stB"], writes=[("yt", 6 + c)])
                k.phase_end()
            with ExitStack() as es:
                sb = lambda n, s, d: es.enter_context(self.sbuf(n, s, d))
                wt = [sb(f"wq{i}", [128, 8, 128], BF16) for i in range(3)]
                wv = sb("wv", [128, 8, 512], BF16)
                NR = 6
                st16 = [sb(f"st16_{i}", [16, 512], F32) for i in range(NR)]
                swp = [sb(f"swp{i}", [16, 512], F32) for i in range(NR)]
                t1 = [sb(f"t1_{i}", [16, 512], F32) for i in range(NR)]
                cs = sb("cs", [16, S], F32)
                sn = sb("sn", [16, S], F32)
                k.dma("sp", cs[:], self.rope[0], writes=["cs"])
                k.dma("sp", sn[:], self.rope[1], writes=["sn"])
                qst = [sb(f"qst{i}", [64, S], BF16) for i in range(4)]
                vst = [sb(f"vst{i}", [128, 512], BF16) for i in range(2)]
                nps = 0
                nrope = 0
                ncs = 0
                nq = 0
                for typ in range(2):
                    for qk_ in range(2):
                        for hp in range(2):
                            c0 = 768 + typ * 768 + qk_ * 256 + 128 * hp
                            wi = nq % 3
                            qa = 2 * (nq % 2)
                            nq += 1
                            self.wload(wt[wi], w, c0, 128, f"wq{wi}")
                            scale = 0.125 if qk_ == 0 else 1.0
                            for tg in range(NTG):
                                b = nps % 6
                                nps += 1
                                self.gemm_fm(self.psb[b][:, :], wt[wi], 128, xnT, tg * 512, 512, f"wq{wi}", f"ps{b}")
                                for hh in range(2):
                                    qi = qa + hh
                                    p0 = 64 * hh
                                    k.op("act", lambda e, b=b, qi=qi, tg=tg, scale=scale, p0=p0: e.activation(
                                        out=qst[qi][:, tg * 512:(tg + 1) * 512], in_=self.psb[b][p0:p0 + 64, :], func=AF.Copy, scale=scale),
                                        reads=[f"ps{b}"], writes=[(f"qst{qi}", tg)])
                                    if typ == 1:
                                        continue
                                    r = nrope % NR
                                    nrope += 1
                                    k.op("act", lambda e, b=b, r=r, scale=scale, p0=p0: e.activation(
                                        out=st16[r][:], in_=self.psb[b][p0:p0 + 16, :], func=AF.Copy, scale=scale),
                                        reads=[f"ps{b}"], writes=[f"st16_{r}"])
                                    k.dma("sp", swp[r][0:8, :], st16[r][8:16, :], reads=[f"st16_{r}"], writes=[f"swp{r}a"])
                                    k.dma("sp", swp[r][8:16, :], st16[r][0:8, :], reads=[f"st16_{r}"], writes=[f"swp{r}b"])
                                    k.op("dve", lambda e, r=r, tg=tg: e.tensor_tensor(out=t1[r][:], in0=swp[r][:], in1=sn[:, tg * 512:(tg + 1) * 512], op=ALU.mult),
                                         reads=[f"swp{r}a", f"swp{r}b", "sn"], writes=[f"t1_{r}"])
                                    k.op("dve", lambda e, r=r, tg=tg: e.tensor_tensor(out=st16[r][:], in0=st16[r][:], in1=cs[:, tg * 512:(tg + 1) * 512], op=ALU.mult),
                                         reads=[f"st16_{r}", "cs", f"swp{r}a", f"swp{r}b"], writes=[f"st16_{r}"])
                                    k.op("dve", lambda e, r=r, qi=qi, tg=tg: e.tensor_tensor(
                                        out=qst[qi][0:16, tg * 512:(tg + 1) * 512], in0=st16[r][:], in1=t1[r][:], op=ALU.add),
                                        reads=[f"st16_{r}", f"t1_{r}", (f"qst{qi}", tg)], writes=[(f"qst{qi}", tg)])
                            for hh in range(2):
                                qi = qa + hh
                                k.dma("sp", self.qk[typ, qk_, 2 * hp + hh, 0:64, :], qst[qi][:], reads=[(f"qst{qi}", tg) for tg in range(NTG)],
                                      writes=[("qk", typ, qk_, hp, hh)])
                src_v = w.rearrange("(kc p) n -> p kc n", p=128)
                k.dma("pool", wv[:, :, 0:256], src_v[:, :, 1280:1536], writes=["wv"])
                k.dma("pool", wv[:, :, 256:512], src_v[:, :, 2048:2304], writes=["wv"])
                for tb in range(NTB):
                    b = nps % 6
                    nps += 1
                    i = tb % 2
                    for kc in range(8):
                        k.op("pe", lambda e, b=b, kc=kc, tb=tb: e.matmul(self.psb[b][:, :], lhsT=xnT[:, kc, tb * 128:(tb + 1) * 128], rhs=wv[:, kc, :],
                                                                         start=(kc == 0), stop=(kc == 7)), reads=["wv"], writes=[f"ps{b}"])
                    k.op("act", lambda e, b=b, i=i: e.copy(out=vst[i][:], in_=self.psb[b][:, :]), reads=[f"ps{b}"], writes=[f"vst{i}"])
                    k.dma("sp", self.vs[0, tb * 128:(tb + 1) * 128, :], vst[i][:, 0:256], reads=[f"vst{i}"], writes=[("vs0", tb)])
                    k.dma("sp", self.vs[1, tb * 128:(tb + 1) * 128, :], vst[i][:, 256:512], reads=[f"vst{i}"], writes=[("vs1", tb)])
                k.phase_end()
            with ExitStack() as es:
                sb = lambda n, s, d: es.enter_context(self.sbuf(n, s, d))
                wf = sb("wf", [128, 8, 4], BF16)
                fl = sb("fl", [4, S], F32)
                cc = sb("cc", [4, S], F32)
                onesf = sb("onesf", [4, S], F32)
                crow = sb("crow", [4, S], BF16)
                negb = sb("negb", [4, 1], F32)
                cnT = sb("cnT", [128, 128], F32)
                nps = 0
                self.wload(wf, w, 2304, 4, "wf")
                k.dma("sp", negb[:], self.b_forget[l].rearrange("(p o) -> p o", o=1), writes=["negb"], allow_slow_non_contiguous=True)
                k.op("dve", lambda e: e.tensor_scalar(out=negb[:], in0=negb[:], scalar1=-1.0, scalar2=None, op0=ALU.mult),
                     reads=["negb"], writes=["negb"])
                k.op("pool", lambda e: e.memset(onesf[:], 1.0), writes=["onesf"])
                for tg in range(NTG):
                    b = nps % 6
                    nps += 1
                    self.gemm_fm(self.psb[b][0:4, :], wf, 4, xnT, tg * 512, 512, "wf", f"ps{b}")
                    k.op("act", lambda e, b=b, tg=tg: e.activation(out=fl[:, tg * 512:(tg + 1) * 512], in_=self.psb[b][0:4, :], func=AF.Exp,
                                                                   scale=-1.0, bias=negb[:, 0:1]), reads=[f"ps{b}", "negb"], writes=["fl"])
                k.op("act", lambda e: e.activation(out=fl[:], in_=fl[:], func=AF.Ln, bias=1.0), reads=["fl"], writes=["fl"])
                k.op("dve", lambda e: e.tensor_scalar(out=fl[:], in0=fl[:], scalar1=-1.0, scalar2=None, op0=ALU.mult), reads=["fl"], writes=["fl"])
                k.op("dve", lambda e: e.tensor_tensor_scan(out=cc[:], data0=onesf[:], data1=fl[:], initial=0.0, op0=ALU.mult, op1=ALU.add),
                     reads=["fl", "onesf"], writes=["cc"])
                k.op("dve", lambda e: e.tensor_copy(out=crow[:], in_=cc[:]), reads=["cc"], writes=["crow"])
                for h in range(4):
                    k.dma("sp", self.qk[1, 0, h, 64:65, :], crow[h:h + 1, :], reads=["crow"], writes=[("qkc", h)])
                k.op("dve", lambda e: e.tensor_copy(out=crow[:], in_=onesf[:]), reads=["onesf", "crow"], writes=["crow"])
                for h in range(4):
                    k.dma("sp", self.qk[1, 1, h, 64:65, :], crow[h:h + 1, :], reads=["crow"], writes=[("qk1", h)])
                pcn = self.psb[6]
                for j in range(NTB):
                    k.op("pe", lambda e, j=j: e.transpose(pcn[:, 4 * j:4 * j + 4], cc[0:4, j * 128:(j + 1) * 128], self.ident_f[0:4, 0:4]),
                         reads=["cc"], writes=["pcn"])
                k.op("dve", lambda e: e.tensor_scalar(out=cnT[:], in0=pcn[:, 0:128], scalar1=-1.0, scalar2=None, op0=ALU.mult),
                     reads=["pcn"], writes=["cnT"])
                k.dma("sp", self.cn, cnT[:], reads=["cnT"], writes=["cn"])
                k.phase_end()

    def phase_attn(self, l):
        nc, k = self.nc, self.k
        with ExitStack() as es:
            sb = lambda n, s, d: es.enter_context(self.sbuf(n, s, d))
            qT = [sb(f"qT{i}", [65, S], BF16) for i in range(2)]
            kT = [sb(f"kT{i}", [65, S], BF16) for i in range(2)]
            vt = [sb(f"vt{i}", [128, NTB, 128], BF16) for i in range(2)]
            oT = [sb(f"oT{i}", [64, S], BF16) for i in range(2)]
            pT = [sb(f"pT{i}", [128, 512], BF16) for i in range(6)]
            rden = [sb(f"rden{i}", [128, 512], F32) for i in range(2)]
            mw = sb("mw", [128, MW_W], BF16)
            cneg = sb("cneg", [128, 128], F32)
            k.dma("pool", mw[:], self.c_mw, writes=["mw"])
            k.dma("sp", cneg[:], self.cn, writes=["cneg"])
            for i in range(2):
                k.op("pool", lambda e, i=i: e.memset(vt[i][:, :, 64:128], 1.0), writes=[f"vt{i}"])
                k.op("pool", lambda e, i=i: e.memset(qT[i][64:65, :], 0.0), writes=[f"qT{i}"])
                k.op("pool", lambda e, i=i: e.memset(kT[i][64:65, :], 0.0), writes=[f"kT{i}"])
            insts = [(t, h) for t in range(2) for h in range(4)]

            def load(n):
                t, h = insts[n]
                i = n % 2
                kr = 65 if t == 1 else 64
                k.dma("sp", qT[i][0:kr, :], self.qk[t, 0, h, 0:kr, :], writes=[f"qT{i}"])
                k.dma("sp", kT[i][0:kr, :], self.qk[t, 1, h, 0:kr, :], writes=[f"kT{i}"])
                k.dma("sp", vt[i][:, :, 0:64], self.vs[t].rearrange("(j p) d -> p j d", p=128)[:, :, 64 * h:64 * (h + 1)], writes=[f"vt{i}"])

            SB = [0, 1, 2, 3, 6]
            LA = 4
            units = []
            nacc = 0
            for n, (t, h) in enumerate(insts):
                for I in range(NTG):
                    jlo = 0 if t == 1 else max(0, 4 * I - 16)
                    a_ = nacc % 2
                    nacc += 1
                    js = list(range(jlo, 4 * I + 4))
                    for idx, j in enumerate(js):
                        c1 = 512
                        if t == 0 and 4 * I - 16 >= 0 and j - (4 * I - 16) < 3 and idx > 0:
                            c1 = 128 * (j - (4 * I - 16) + 1)
                        units.append(dict(n=n, t=t, h=h, I=I, j=j, idx=idx, last=(idx == len(js) - 1), a=a_, c1=c1,
                                          first_of_inst=(I == 0 and idx == 0), last_of_inst=(I == NTG - 1 and idx == len(js) - 1)))

            pending = []

            def emit_s(u, m):
                i = u["n"] % 2
                kr = 65
                I, j = u["I"], u["j"]
                r = j - 4 * I
                c0 = 128 * r if r > 0 else 0
                sbk = SB[m % len(SB)]
                ps_s = self.psb[sbk]
                c1 = u["c1"]
                k.op("pe", lambda e: e.matmul(ps_s[:, c0:c1], lhsT=kT[i][0:kr, j * 128:(j + 1) * 128],
                                              rhs=qT[i][0:kr, I * 512 + c0:I * 512 + c1], start=True, stop=True),
                     reads=[f"kT{i}", f"qT{i}"], writes=[f"ps{sbk}"])

            def emit_rest(u, m):
                i = u["n"] % 2
                t, h, I, j, idx, a_ = u["t"], u["h"], u["I"], u["j"], u["idx"], u["a"]
                r = j - 4 * I
                c0 = 128 * r if r > 0 else 0
                sbk = SB[m % len(SB)]
                ps_s = self.psb[sbk]
                p = m % len(pT)
                c1 = u["c1"]
                oacc = self.psb[4 + a_]
                if u["first_of_inst"] and u["n"] + 1 < len(insts):
                    load(u["n"] + 1)
                if t == 1:
                    k.op("act", lambda e: e.activation(out=pT[p][:, c0:512], in_=ps_s[:, c0:512], func=AF.Exp,
                                                       bias=cneg[:, 4 * j + h:4 * j + h + 1], scale=1.0),
                         reads=[f"ps{sbk}", "cneg"], writes=[f"pT{p}"])
                    if r >= 0:
                        k.op("pool", lambda e: e.affine_select(out=pT[p][:, c0:c0 + 128], in_=pT[p][:, c0:c0 + 128], pattern=[[1, 128]],
                                                               compare_op=ALU.is_ge, fill=0.0, base=0, channel_multiplier=-1),
                             reads=[f"pT{p}"], writes=[f"pT{p}"])
                else:
                    k.op("act", lambda e: e.activation(out=pT[p][:, c0:c1], in_=ps_s[:, c0:c1], func=AF.Exp),
                         reads=[f"ps{sbk}"], writes=[f"pT{p}"])
                    m0 = 512 * I - 128 * j + MW_OFF + c0
                    meng = "dve"
                    k.op(meng, lambda e: e.tensor_tensor(out=pT[p][:, c0:c1], in0=pT[p][:, c0:c1], in1=mw[:, m0:m0 + c1 - c0], op=ALU.mult),
                         reads=[f"pT{p}", "mw"], writes=[f"pT{p}"])
                k.op("pe", lambda e: e.matmul(oacc[:, c0:c1], lhsT=vt[i][:, j, :], rhs=pT[p][:, c0:c1], start=(idx == 0), stop=u["last"]),
                     reads=[f"vt{i}", f"pT{p}"], writes=[f"po{a_}"])
                if u["last"]:
                    if t == 0:
                        k.op("act", lambda e: e.activation(out=rden[a_][64:128, :], in_=oacc[64:128, :], func=AF.Ln),
                             reads=[f"po{a_}"], writes=[f"rden{a_}"])
                        k.op("act", lambda e: e.activation(out=rden[a_][64:128, :], in_=rden[a_][64:128, :], func=AF.Exp, scale=-1.0),
                             reads=[f"rden{a_}"], writes=[f"rden{a_}"])
                    else:
                        k.op("dve", lambda e: e.reciprocal(out=rden[a_][64:128, :], in_=oacc[64:128, :]), reads=[f"po{a_}"], writes=[f"rden{a_}"])
                    k.op("dve", lambda e: e.tensor_tensor(out=oT[i][:, I * 512:(I + 1) * 512], in0=oacc[0:64, :], in1=rden[a_][64:128, :], op=ALU.mult),
                         reads=[f"po{a_}", f"rden{a_}"], writes=[f"oT{i}"])
                if u["last_of_inst"]:
                    row0 = 256 + 256 * t + 64 * h
                    k.dma("sp", self.yt[row0:row0 + 64, :], oT[i][:], reads=[f"oT{i}"], writes=[("yt", "a", t, h)])

            load(0)
            nu = len(units)
            for m in range(nu + LA):
                if m < nu:
                    emit_s(units[m], m)
                if m - LA >= 0:
                    emit_rest(units[m - LA], m - LA)
            k.phase_end()

    def phase_outproj(self, srcT, nkc, w2d, h_src, norm_g=None, final=False, pieces_fn=None):
        nc, k = self.nc, self.k
        xnT = self.xnT
        with ExitStack() as es:
            sb = lambda n, s, d: es.enter_context(self.sbuf(n, s, d))
            wo = sb("wo", [128, nkc, D], BF16)
            yt = [sb(f"ytb{i}", [128, nkc, 512], BF16) for i in range(2)]
            NHB = 4
            hb = [sb(f"hbo{i}", [128, D], F32) for i in range(NHB)]
            gb = sb("gbo", [128, D], F32)
            junk = sb("junko", [128, D], BF16)
            ss = [sb(f"sso{i}", [128, 1], F32) for i in range(2)]
            rs = [sb(f"rso{i}", [128, 1], F32) for i in range(2)]
            if final:
                ob = [sb(f"obo{i}", [128, D], F32) for i in range(2)]
            else:
                xn = [sb(f"xno{i}", [128, D], BF16) for i in range(2)]
            k.dma("sp", gb[:], bcast_rows(norm_g, 128), writes=["gb"])
            wsrc = w2d.rearrange("(kc p) n -> p kc n", p=128)
            step = 4 if nkc <= 8 else 2
            for c in range(0, nkc, step):
                k.dma("pool", wo[:, c:c + step, :], wsrc[:, c:c + step, :], writes=[("wo", c)])
            wkeys = [("wo", c) for c in range(0, nkc, step)]
            ysrc = srcT.rearrange("(kc p) t -> p kc t", p=128)
            k.dma("sp", yt[0][:], ysrc[:, :, 0:512], writes=["ytb0"])

            def loadh(tb):
                k.dma("sp", hb[tb % NHB][:], h_src[tb * 128:(tb + 1) * 128, :], writes=[f"hbo{tb % NHB}"])

            def norm_a1(tb):
                i = tb % 2
                hi = tb % NHB
                k.op("act", lambda e: e.activation(out=junk[:], in_=hb[hi][:], func=AF.Square, accum_out=ss[i][:, 0:1]),
                     reads=[f"hbo{hi}"], writes=["junk", f"ss{i}"])
                k.op("act", lambda e: e.activation(out=ss[i][:], in_=ss[i][:], func=AF.Sqrt, bias=EPS, scale=1.0 / D),
                     reads=[f"ss{i}"], writes=[f"ss{i}"])

            def norm_a2(tb):
                i = tb % 2
                hi = tb % NHB
                k.op("dve", lambda e: e.reciprocal(out=rs[i][:], in_=ss[i][:]), reads=[f"ss{i}"], writes=[f"rs{i}"])
                if final:
                    k.op("dve", lambda e: e.scalar_tensor_tensor(out=ob[i][:], in0=hb[hi][:], scalar=rs[i][:, 0:1], in1=gb[:],
                                                                 op0=ALU.mult, op1=ALU.mult),
                         reads=[f"hbo{hi}", f"rs{i}", "gb"], writes=[f"ob{i}"])
                    k.dma("act", self.out[tb * 128:(tb + 1) * 128, :], ob[i][:], reads=[f"ob{i}"], writes=[("out", tb)])
                else:
                    k.op("dve", lambda e: e.scalar_tensor_tensor(out=xn[i][:], in0=hb[hi][:], scalar=rs[i][:, 0:1], in1=gb[:],
                                                                 op0=ALU.mult, op1=ALU.mult),
                         reads=[f"hbo{hi}", f"rs{i}", "gb"], writes=[f"xn{i}"])

            def norm_b(tb):
                if final:
                    return
                i = tb % 2
                for kc in range(8):
                    k.op("pe", lambda e, kc=kc: e.transpose(self.pst[:, kc * 128:(kc + 1) * 128], xn[i][:, kc * 128:(kc + 1) * 128],
                                                            self.ident_bf[:]),
                         reads=[f"xn{i}"], writes=["pst"])
                k.op("act", lambda e: e.copy(out=xnT[:, :, tb * 128:(tb + 1) * 128], in_=self.pst[:].rearrange("p (k t) -> p k t", k=8)),
                     reads=["pst"], writes=[("xnT", tb)])

            loadh(0)
            loadh(1)
            pieces = pieces_fn(es) if pieces_fn is not None else []
            nb = 0
            for tg in range(NTG):
                yi = tg % 2
                if tg + 1 < NTG:
                    k.dma("sp", yt[1 - yi][:], ysrc[:, :, (tg + 1) * 512:(tg + 2) * 512], writes=[f"ytb{1 - yi}"])
                for tbi in range(4):
                    tb = 4 * tg + tbi
                    hi = tb % NHB
                    if tb + 2 < NTB:
                        loadh(tb + 2)
                    if pieces and tb >= 2 and tb % 2 == 0:
                        pieces.pop(0)()
                    for half in range(2):
                        b = nb % 7
                        nb += 1
                        for kc in range(nkc):
                            k.op("pe", lambda e, b=b, yi=yi, kc=kc, tbi=tbi, half=half: e.matmul(
                                self.psb[b][:, :], lhsT=yt[yi][:, kc, tbi * 128:(tbi + 1) * 128], rhs=wo[:, kc, half * 512:(half + 1) * 512],
                                start=(kc == 0), stop=(kc == nkc - 1)), reads=[f"ytb{yi}"] + (wkeys if kc == 0 else []), writes=[f"ps{b}"])
                        k.op("dve", lambda e, b=b, hi=hi, half=half: e.tensor_tensor(
                            out=hb[hi][:, half * 512:(half + 1) * 512], in0=self.psb[b][:, :], in1=hb[hi][:, half * 512:(half + 1) * 512], op=ALU.add),
                            reads=[f"ps{b}", f"hbo{hi}"], writes=[f"hbo{hi}"])
                    if not final:
                        k.dma("act", self.hs[tb * 128:(tb + 1) * 128, :], hb[hi][:], reads=[f"hbo{hi}"], writes=[("hs", tb)])
                    norm_a1(tb)
                    if tb >= 1:
                        norm_a2(tb - 1)
                    if tb >= 2:
                        norm_b(tb - 2)
            norm_a2(NTB - 1)
            norm_b(NTB - 2)
            norm_b(NTB - 1)
            while pieces:
                pieces.pop(0)()
            k.phase_end()

    def xa_kv_pieces(self, l, es):
        nc, k = self.nc, self.k
        sb = lambda n, s, d: es.enter_context(self.sbuf(n, s, d))
        mT = sb("mT", [128, 8, MEM], BF16)
        wk = [sb(f"wk{i}", [128, 8, 128], BF16) for i in range(2)]
        wv = sb("wvx", [128, 8, D], BF16)
        w = self.w_xkv[l]
        xk = [("kxnT", 0), ("kxnT", 1)]
        pieces = []
        pieces.append(lambda: self.emit_norm(es, self.mem, self.g_mem[l], mT, 2, "k"))

        def kpiece(fc):
            i = fc % 2
            self.wload(wk[i], w, fc * 128, 128, f"wk{i}")
            b = fc % 4
            self.gemm_fm(self.psb[b][:, 0:MEM], wk[i], 128, mT, 0, MEM, f"wk{i}", f"ps{b}", xkeys=xk)
            k.op("act", lambda e: e.copy(out=self.kT_sb[:, fc, :], in_=self.psb[b][:, 0:MEM]), reads=[f"ps{b}"], writes=["kT_sb"])

        for fc in range(8):
            pieces.append(lambda fc=fc: kpiece(fc))

        def vload():
            wsrc = w.rearrange("(kc p) n -> p kc n", p=128)
            k.dma("pool", wv[:, :, 0:512], wsrc[:, :, D:D + 512], writes=["wvx0"])
            k.dma("pool", wv[:, :, 512:1024], wsrc[:, :, D + 512:2 * D], writes=["wvx1"])

        pieces.append(vload)

        def vpiece(mb, half):
            b = 4 + (2 * mb + half) % 2
            for kc in range(8):
                k.op("pe", lambda e, kc=kc: e.matmul(self.psb[b][:, :], lhsT=mT[:, kc, mb * 128:(mb + 1) * 128],
                                                     rhs=wv[:, kc, half * 512:(half + 1) * 512], start=(kc == 0), stop=(kc == 7)),
                     reads=[f"wvx{half}"] + xk, writes=[f"ps{b}"])
            k.op("act", lambda e: e.copy(out=self.v_sb[:, mb, half * 512:(half + 1) * 512], in_=self.psb[b][:, :]),
                 reads=[f"ps{b}"], writes=["v_sb"])

        for mb in range(2):
            for half in range(2):
                pieces.append(lambda mb=mb, half=half: vpiece(mb, half))
        return pieces

    def phase_xa_kv(self, l):
        pass

    def phase_xa(self, l):
        nc, k = self.nc, self.k
        with ExitStack() as es0:
            xnT = self.xnT
            with ExitStack() as es:
                sb = lambda n, s, d: es.enter_context(self.sbuf(n, s, d))
                wq = [sb(f"wxq{i}", [128, 8, 128], BF16) for i in range(2)]
                qTh = [sb(f"qTh{i}", [128, 2, S], BF16) for i in range(2)]
                xo = [sb(f"xo{i}", [128, 2, S], BF16) for i in range(2)]
                pT = [sb(f"pTx{i}", [128, 512], BF16) for i in range(4)]
                rden = [sb(f"rdx{i}", [128, 512], F32) for i in range(2)]
                w = self.w_xq[l]
                nps = 0
                npt = 0
                nr = 0
                nbo = 0
                for hd in range(4):
                    qi = hd % 2
                    for dc in range(2):
                        fc = 2 * hd + dc
                        self.wload(wq[dc], w, fc * 128, 128, f"wxq{dc}")
                        for tg in range(NTG):
                            b = nps % 4
                            nps += 1
                            self.gemm_fm(self.psb[b][:, :], wq[dc], 128, xnT, tg * 512, 512, f"wxq{dc}", f"ps{b}")
                            k.op("act", lambda e, b=b, qi=qi, dc=dc, tg=tg: e.activation(
                                out=qTh[qi][:, dc, tg * 512:(tg + 1) * 512], in_=self.psb[b][:, :], func=AF.Copy, scale=1.0 / 16.0),
                                reads=[f"ps{b}"], writes=[(f"qTh{qi}", dc)])
                    ptss = {}

                    def xa_s(tg, hd=hd, qi=qi):
                        nonlocal nps, npt
                        pts = []
                        for mb in range(2):
                            b = nps % 4
                            nps += 1
                            p = npt % 4
                            npt += 1
                            pts.append(p)
                            for dc in range(2):
                                k.op("pe", lambda e, b=b, dc=dc, mb=mb: e.matmul(
                                    self.psb[b][:, :], lhsT=self.kT_sb[:, 2 * hd + dc, mb * 128:(mb + 1) * 128],
                                    rhs=qTh[qi][:, dc, tg * 512:(tg + 1) * 512], start=(dc == 0), stop=(dc == 1)),
                                    reads=[(f"qTh{qi}", 0), (f"qTh{qi}", 1)], writes=[f"ps{b}"])
                            k.op("act", lambda e, b=b, p=p: e.activation(out=pT[p][:], in_=self.psb[b][:, :], func=AF.Exp),
                                 reads=[f"ps{b}"], writes=[f"pTx{p}"])
                        ptss[tg] = pts

                    def xa_rest(tg, hd=hd, qi=qi):
                        nonlocal nr, nbo
                        pts = ptss[tg]
                        pden = self.psb[6]
                        for mb in range(2):
                            k.op("pe", lambda e, mb=mb, p=pts[mb]: e.matmul(pden[:, :], lhsT=self.ones_bf[:, :], rhs=pT[p][:],
                                                                            start=(mb == 0), stop=(mb == 1)), reads=[f"pTx{pts[mb]}"], writes=["pden"])
                        ri = nr % 2
                        nr += 1
                        k.op("act", lambda e, ri=ri: e.activation(out=rden[ri][:], in_=pden[:, :], func=AF.Ln), reads=["pden"], writes=[f"rdx{ri}"])
                        k.op("act", lambda e, ri=ri: e.activation(out=rden[ri][:], in_=rden[ri][:], func=AF.Exp, scale=-1.0),
                             reads=[f"rdx{ri}"], writes=[f"rdx{ri}"])
                        for dc in range(2):
                            bo = 4 + (nbo % 2)
                            nbo += 1
                            for mb in range(2):
                                k.op("pe", lambda e, bo=bo, mb=mb, dc=dc, p=pts[mb]: e.matmul(
                                    self.psb[bo][:, :], lhsT=self.v_sb[:, mb, (2 * hd + dc) * 128:(2 * hd + dc + 1) * 128], rhs=pT[p][:],
                                    start=(mb == 0), stop=(mb == 1)), reads=[f"pTx{pts[mb]}"], writes=[f"ps{bo}"])
                            k.op("dve", lambda e, bo=bo, ri=ri, dc=dc: e.tensor_tensor(
                                out=xo[qi][:, dc, tg * 512:(tg + 1) * 512], in0=self.psb[bo][:, :], in1=rden[ri][:], op=ALU.mult),
                                reads=[f"ps{bo}", f"rdx{ri}"], writes=[f"xo{qi}"])

                    xa_s(0)
                    for tg in range(NTG):
                        if tg + 1 < NTG:
                            xa_s(tg + 1)
                        xa_rest(tg)
                    k.dma("sp", self.yt[256 * hd:256 * (hd + 1), :].rearrange("(c p) t -> p c t", p=128), xo[qi][:],
                          reads=[f"xo{qi}"], writes=[("yt", hd)])
                k.phase_end()

    def phase_ffn_up(self, l):
        nc, k = self.nc, self.k
        with ExitStack() as es0:
            xnT = self.xnT
            with ExitStack() as es:
                sb = lambda n, s, d: es.enter_context(self.sbuf(n, s, d))
                wa = [sb(f"wa{i}", [128, 8, 128], BF16) for i in range(2)]
                wg = [sb(f"wg{i}", [128, 8, 128], BF16) for i in range(2)]
                CW = S + 2
                ca = [sb(f"ca{i}", [128, CW], F32) for i in range(2)]
                cg = [sb(f"cg{i}", [128, CW], F32) for i in range(2)]
                ao = [sb(f"ao{i}", [128, S], BF16) for i in range(2)]
                taps = sb("taps", [128, 3, 44], F32)
                w = self.w_up[l]
                for t_ in range(3):
                    k.dma("sp", taps[:, t_, :], self.w_ffconv[l, t_].rearrange("(c p) -> p c", p=128), writes=[("taps", t_)],
                          allow_slow_non_contiguous=True)
                tk = [("taps", t_) for t_ in range(3)]
                nps = 0
                for i in range(22):
                    wi = i % 2
                    self.wload(wa[wi], w, 128 * i, 128, f"wa{wi}")
                    self.wload(wg[wi], w, DFF + 128 * i, 128, f"wg{wi}")
                    bufs = ((ca[wi], f"ca{wi}", wa[wi], f"wa{wi}", i), (cg[wi], f"cg{wi}", wg[wi], f"wg{wi}", 22 + i))
                    for (c_, cn_, w_, wn_, ci) in bufs:
                        k.op("dve", lambda e, c_=c_: e.memset(c_[:, 0:2], 0.0), writes=[(cn_, -1, "A")])
                    for tg in range(NTG):
                        T0 = tg * 512
                        for (c_, cn_, w_, wn_, ci) in bufs:
                            b = nps % 7
                            nps += 1
                            ps = self.psb[b]
                            self.gemm_fm(ps[:, :], w_, 128, xnT, T0, 512, wn_, f"ps{b}")
                            k.op("act", lambda e, c_=c_, ps=ps, ci=ci, T0=T0: e.activation(out=c_[:, T0 + 2:T0 + 514], in_=ps[:, :], func=AF.Copy,
                                                                                          scale=taps[:, 0, ci:ci + 1]),
                                 reads=[f"ps{b}"] + tk, writes=[(cn_, tg, "A")])
                            k.op("dve", lambda e, c_=c_, ps=ps, ci=ci, T0=T0: e.scalar_tensor_tensor(
                                out=c_[:, T0 + 1:T0 + 513], in0=ps[:, :], scalar=taps[:, 1, ci:ci + 1], in1=c_[:, T0 + 1:T0 + 513],
                                op0=ALU.mult, op1=ALU.add),
                                reads=[f"ps{b}", (cn_, tg, "A"), (cn_, tg - 1, "A"), (cn_, tg - 1, "D")] + tk, writes=[(cn_, tg, "D")])
                            k.op("dve", lambda e, c_=c_, ps=ps, ci=ci, T0=T0: e.scalar_tensor_tensor(
                                out=c_[:, T0:T0 + 512], in0=ps[:, :], scalar=taps[:, 2, ci:ci + 1], in1=c_[:, T0:T0 + 512],
                                op0=ALU.mult, op1=ALU.add),
                                reads=[f"ps{b}", (cn_, tg, "D"), (cn_, tg - 1, "D"), (cn_, tg - 1, "A")] + tk, writes=[(cn_, tg, "D")])
                    allk = lambda cn_: [(cn_, tg, "A") for tg in range(-1, NTG)] + [(cn_, tg, "D") for tg in range(NTG)]
                    k.op("act", lambda e, wi=wi: e.activation(out=cg[wi][:, 0:S], in_=cg[wi][:, 0:S], func=AF.Silu),
                         reads=allk(f"cg{wi}"), writes=[(f"cg{wi}", "s")])
                    k.op("dve", lambda e, wi=wi: e.tensor_tensor(out=ao[wi][:], in0=ca[wi][:, 0:S], in1=cg[wi][:, 0:S], op=ALU.mult),
                         reads=[(f"cg{wi}", "s")] + allk(f"ca{wi}") + allk(f"cg{wi}"), writes=[f"ao{wi}"])
                    k.dma("sp", self.actt[128 * i:128 * (i + 1), :], ao[wi][:], reads=[f"ao{wi}"], writes=[("actt", i)])
                k.phase_end()

    def phase_final(self):
        nc, k = self.nc, self.k
        with ExitStack() as es:
            sb = lambda n, s, d: es.enter_context(self.sbuf(n, s, d))
            gb = sb("gbF", [128, D], F32)
            hb = [sb(f"hbF{i}", [128, D], F32) for i in range(2)]
            ob = [sb(f"obF{i}", [128, D], F32) for i in range(2)]
            junk = sb("junkF", [128, D], BF16)
            ss = [sb(f"ssF{i}", [128, 1], F32) for i in range(2)]
            rs = [sb(f"rsF{i}", [128, 1], F32) for i in range(2)]
            k.dma("sp", gb[:], bcast_rows(self.g_final, 128), writes=["gb"])
            hb.append(sb("hbF2", [128, D], F32))
            ob.append(sb("obF2", [128, D], F32))

            def loadf(tb):
                k.dma("sp", hb[tb % 3][:], self.hs[tb * 128:(tb + 1) * 128, :], writes=[f"hb{tb % 3}"])

            loadf(0)
            loadf(1)
            for tb in range(NTB):
                i = tb % 2
                hi = tb % 3
                if tb + 2 < NTB:
                    loadf(tb + 2)
                k.op("act", lambda e, i=i, hi=hi: e.activation(out=junk[:], in_=hb[hi][:], func=AF.Square, accum_out=ss[i][:, 0:1]),
                     reads=[f"hb{hi}"], writes=["junk", f"ss{i}"])
                k.op("act", lambda e, i=i: e.activation(out=ss[i][:], in_=ss[i][:], func=AF.Sqrt, bias=EPS, scale=1.0 / D),
                     reads=[f"ss{i}"], writes=[f"ss{i}"])
                k.op("dve", lambda e, i=i: e.reciprocal(out=rs[i][:], in_=ss[i][:]), reads=[f"ss{i}"], writes=[f"rs{i}"])
                k.op("dve", lambda e, i=i, hi=hi: e.scalar_tensor_tensor(out=ob[hi][:], in0=hb[hi][:], scalar=rs[i][:, 0:1], in1=gb[:],
                                                                        op0=ALU.mult, op1=ALU.mult),
                     reads=[f"hb{hi}", f"rs{i}", "gb"], writes=[f"ob{hi}"])
                k.dma("pool", self.out[tb * 128:(tb + 1) * 128, :], ob[hi][:], reads=[f"ob{hi}"], writes=[("out", tb)])
            k.phase_end()


def host_constants():
    ident = np.eye(128, dtype=np.float32)
    xs = np.arange(MW_W)[None, :] - MW_OFF - np.arange(128)[:, None]
    m = ((xs >= 0) & (xs <= 128)).astype(np.float32)
    m += ((xs >= 0) & (xs % 4 == 0) & (xs <= 512)).astype(np.float32)
    m += ((xs >= 0) & (xs % 16 == 0) & (xs <= 2048)).astype(np.float32)
    invf = (500000.0 ** (-np.arange(0, 16, 2, dtype=np.float32) / 16.0)).astype(np.float32)
    c_invf = np.zeros((16, 2), np.float32)
    c_invf[0:8, 0] = invf
    c_invf[8:16, 0] = invf
    c_invf[0:8, 1] = -1.0
    c_invf[8:16, 1] = 1.0
    rc = np.zeros((128, 2, 16), np.float32)
    t = np.arange(16) + 1
    for c, (wa, wb) in enumerate(((2, 4), (8, 16))):
        rc[0:64, c, :] = 1.0 / np.minimum(t, wa)
        rc[64:128, c, :] = 1.0 / np.minimum(t, wb)
    return {"c_ident": ident, "c_mw": m.astype(np.float32), "c_invf": c_invf, "c_rc": rc}


_CACHE = {}


def kernel(**inputs):
    n = 8
    if "nc" not in _CACHE:
        _CACHE["nc"] = Builder().build()
    nc = _CACHE["nc"]
    consts = host_constants()
    shared = {}
    for name in ("g_mix", "w_in", "b_forget", "w_sconv", "w_pool", "pool_scale", "w_out", "g_xa", "g_mem", "w_xq",
                 "w_xkv", "w_xo", "g_ffn", "w_up", "w_ffconv", "w_down", "g_final"):
        shared[name] = np.ascontiguousarray(np.asarray(inputs[name], dtype=np.float32))
    shared.update(consts)
    x = np.asarray(inputs["x"], dtype=np.float32)
    mem = np.asarray(inputs["mem"], dtype=np.float32)
    pos = np.asarray(inputs["positions"], dtype=np.int32)
    in_maps = []
    for c in range(n):
        m = dict(shared)
        m["x"] = np.ascontiguousarray(x[c])
        m["mem"] = np.ascontiguousarray(mem[c])
        m["positions"] = np.ascontiguousarray(pos[c:c + 1])
        in_maps.append(m)
    res = run_bass_kernel_spmd(nc, in_maps, core_ids=list(range(n)))
    return np.stack([np.asarray(r["out"], dtype=np.float32) for r in res.results], axis=0)
```
